# Optimizing a Trainium2 kernel written in Bass

```python
import jax, jax.numpy as jnp
from jax import lax
import numpy as np

D_MODEL = 2048
BATCH = 8
SEQ = 4096
DEPTH = 4

PLE_DIM = 256
CONV_WIDTH = 31
HEAD_DIM = D_MODEL // 16
CONV_HEADS = 8
POOL_GROUPS = 4
FOURIER_HEADS = 4
CONV_DIM = CONV_HEADS * HEAD_DIM
POOL_DIM = POOL_GROUPS * HEAD_DIM
FOURIER_DIM = FOURIER_HEADS * HEAD_DIM
MIX_DIM = CONV_DIM + POOL_DIM + FOURIER_DIM
IN_DIM = 2 * CONV_DIM + POOL_DIM + FOURIER_DIM
POOL_WINDOWS = (2, 4, 8, 16)
D_FF = -(-8 * D_MODEL // (3 * 256)) * 256
ALPHA = (2 * DEPTH) ** 0.25
BETA = (8 * DEPTH) ** -0.25
LN_EPS = 1e-5

kernel_name = "hybrid_conv_pool_fourier_deepnorm_encoder"


def layer_norm(x, g, b):
    xf = x.astype(jnp.float32)
    mu = jnp.mean(xf, axis=-1, keepdims=True)
    var = jnp.mean(jnp.square(xf - mu), axis=-1, keepdims=True)
    y = (xf - mu) * lax.rsqrt(var + LN_EPS) * g.astype(jnp.float32) + b.astype(jnp.float32)
    return y.astype(x.dtype)


def conformer_conv(val, gate, w_dw, b_dw, ln_g, ln_b):
    u = val * jax.nn.sigmoid(gate)
    y = lax.conv_general_dilated(
        u, w_dw[:, None, :].astype(u.dtype), window_strides=(1,),
        padding=[(CONV_WIDTH // 2, CONV_WIDTH // 2)],
        dimension_numbers=("NWC", "WIO", "NWC"),
        feature_group_count=CONV_DIM) + b_dw
    return jax.nn.silu(layer_norm(y, ln_g, ln_b))


def multiscale_pool(u, w_pool, b_pool, scale):
    B, S, _ = u.shape
    ug = u.reshape(B, S, POOL_GROUPS, HEAD_DIM).astype(jnp.float32)
    t = jnp.arange(S)
    outs = []
    for g, k in enumerate(POOL_WINDOWS):
        xg = ug[:, :, g, :]
        cs = jnp.concatenate([jnp.zeros((B, 1, HEAD_DIM), jnp.float32),
                              jnp.cumsum(xg, axis=1)], axis=1)
        lo = jnp.clip(t - k // 2, 0, S)
        hi = jnp.clip(t + k // 2, 0, S)
        cnt = (hi - lo).astype(jnp.float32)[None, :, None]
        mean = (jnp.take(cs, hi, axis=1) - jnp.take(cs, lo, axis=1)) / cnt
        outs.append(mean - xg)
    pooled = jnp.stack(outs, axis=2).astype(u.dtype)
    y = jnp.einsum("bsgc,gcd->bsgd", pooled, w_pool) + b_pool
    y = y * scale.reshape(POOL_GROUPS, HEAD_DIM)
    return y.reshape(B, S, POOL_DIM)


def fourier_mix(u, w_f, b_f):
    B, S, _ = u.shape
    uh = u.reshape(B, S, FOURIER_HEADS, HEAD_DIM).astype(jnp.float32)
    zf = jnp.fft.fftn(uh, axes=(1, 3), norm="ortho").real.astype(u.dtype)
    y = jnp.einsum("bshc,hcd->bshd", zf, w_f) + b_f
    return y.reshape(B, S, FOURIER_DIM)


def setup_inputs(seed: int = 0) -> dict:
    key = jax.random.key(seed)
    ks = jax.random.split(key, 32)
    f32 = jnp.float32

    def nrm(k, shape, scale):
        return jax.random.normal(k, shape, f32) * scale

    L = DEPTH
    return {
        "x": nrm(ks[0], (BATCH, SEQ, D_MODEL), 1.0),
        "p": nrm(ks[1], (DEPTH, BATCH, SEQ, PLE_DIM), 1.0),
        "w_in": nrm(ks[2], (L, D_MODEL, IN_DIM), D_MODEL ** -0.5),
        "b_in": nrm(ks[3], (L, IN_DIM), 0.02),
        "w_dw": nrm(ks[4], (L, CONV_WIDTH, CONV_DIM), CONV_WIDTH ** -0.5),
        "b_dw": nrm(ks[5], (L, CONV_DIM), 0.02),
        "conv_ln_g": 1.0 + nrm(ks[6], (L, CONV_DIM), 0.02),
        "conv_ln_b": nrm(ks[7], (L, CONV_DIM), 0.02),
        "w_pool": nrm(ks[8], (L, POOL_GROUPS, HEAD_DIM, HEAD_DIM), HEAD_DIM ** -0.5),
        "b_pool": nrm(ks[9], (L, POOL_GROUPS, HEAD_DIM), 0.02),
        "pool_scale": 1.0 + nrm(ks[10], (L, POOL_DIM), 0.02),
        "w_fourier": nrm(ks[11], (L, FOURIER_HEADS, HEAD_DIM, HEAD_DIM), HEAD_DIM ** -0.5),
        "b_fourier": nrm(ks[12], (L, FOURIER_HEADS, HEAD_DIM), 0.02),
        "w_out": nrm(ks[13], (L, MIX_DIM, D_MODEL), BETA * MIX_DIM ** -0.5),
        "b_out": nrm(ks[14], (L, D_MODEL), 0.02),
        "ln1_g": 1.0 + nrm(ks[15], (L, D_MODEL), 0.02),
        "ln1_b": nrm(ks[16], (L, D_MODEL), 0.02),
        "w_gate": nrm(ks[17], (L, D_MODEL, D_FF), D_MODEL ** -0.5),
        "w_up": nrm(ks[18], (L, D_MODEL, D_FF), D_MODEL ** -0.5),
        "w_down": nrm(ks[19], (L, D_FF, D_MODEL), BETA * D_FF ** -0.5),
        "w_ple": nrm(ks[20], (L, PLE_DIM, D_MODEL), BETA * PLE_DIM ** -0.5),
        "w_ple_gate": nrm(ks[21], (L, D_MODEL, D_MODEL), D_MODEL ** -0.5),
        "ln2_g": 1.0 + nrm(ks[22], (L, D_MODEL), 0.02),
        "ln2_b": nrm(ks[23], (L, D_MODEL), 0.02),
    }


def reference(x, p, w_in, b_in, w_dw, b_dw, conv_ln_g, conv_ln_b, w_pool, b_pool,
              pool_scale, w_fourier, b_fourier, w_out, b_out, ln1_g, ln1_b,
              w_gate, w_up, w_down, w_ple, w_ple_gate, ln2_g, ln2_b):
    s1 = CONV_DIM
    s2 = 2 * CONV_DIM
    s3 = 2 * CONV_DIM + POOL_DIM
    for i in range(DEPTH):
        z = jnp.einsum("bsd,de->bse", x, w_in[i]) + b_in[i]
        y_conv = conformer_conv(z[..., :s1], z[..., s1:s2], w_dw[i], b_dw[i],
                                conv_ln_g[i], conv_ln_b[i])
        y_pool = multiscale_pool(z[..., s2:s3], w_pool[i], b_pool[i], pool_scale[i])
        y_four = fourier_mix(z[..., s3:], w_fourier[i], b_fourier[i])
        mixed = jnp.concatenate([y_conv, y_pool, y_four], axis=-1)
        mix_out = jnp.einsum("bsm,md->bsd", mixed, w_out[i]) + b_out[i]
        x = layer_norm(ALPHA * x + mix_out, ln1_g[i], ln1_b[i])
        hid = jax.nn.silu(jnp.einsum("bsd,df->bsf", x, w_gate[i])) * \
            jnp.einsum("bsd,df->bsf", x, w_up[i])
        ffn = jnp.einsum("bsf,fd->bsd", hid, w_down[i])
        ple = jax.nn.sigmoid(jnp.einsum("bsd,de->bse", x, w_ple_gate[i])) * \
            jnp.einsum("bsq,qd->bsd", p[i], w_ple[i])
        x = layer_norm(ALPHA * x + ffn + ple, ln2_g[i], ln2_b[i])
    return x
```

```python
import numpy as np
import ml_dtypes
from contextlib import ExitStack
import concourse.bass as bass
import concourse.mybir as mybir
from concourse.bass_utils import run_bass_kernel_spmd

F32 = mybir.dt.float32
BF16 = mybir.dt.bfloat16
AF = mybir.ActivationFunctionType
ALU = mybir.AluOpType

S = 4096
D = 2048
L = 4
TB = 512
NB = S // TB
DFF = 5632
NF = DFF // 128
PLE = 256
ALPHA = float((2 * L) ** 0.25)
EPS = 1e-5
KW = 31
POOL_K = (2, 4, 8, 16)

PC = {}
_o = 0
for _n, _c in (("b_in", 24), ("b_dw", 8), ("cg", 8), ("cb", 8), ("b_pool", 4), ("pscale", 4), ("b_f", 4),
               ("b_out", 16), ("g1", 16), ("b1", 16), ("g2", 16), ("b2", 16), ("wdw", KW * 8)):
    PC[_n] = _o
    _o += _c
NPC = _o


class Buf:
    __slots__ = ("w", "r")

    def __init__(self):
        self.w = None
        self.r = {}


class Sem:
    __slots__ = ("h", "n")

    def __init__(self, h):
        self.h = h
        self.n = 0


class KB:
    def __init__(self, nc, stack):
        self.nc = nc
        self.stack = stack
        self.all_sems = []
        self.engs = {"pe": nc.tensor, "act": nc.scalar, "dve": nc.vector, "pool": nc.gpsimd, "sp": nc.sync}
        self.esem = {e: self.new_sem() for e in self.engs}
        self.pe_sems = {id(self.esem["pe"])}
        self.waited = {e: {} for e in self.engs}
        self.dsems = {"sp": [self.new_sem() for _ in range(20)], "pool": [self.new_sem() for _ in range(12)],
                      "act": [self.new_sem() for _ in range(16)]}
        self.didx = {q: 0 for q in self.dsems}
        self.nins = 0

    def new_sem(self):
        h = self.stack.enter_context(self.nc.semaphore(f"s{len(self.all_sems)}"))
        s = Sem(h)
        self.all_sems.append(s)
        return s

    def _waits(self, eng, reads, writes, extra=()):
        e = self.engs[eng]
        wd = self.waited[eng]
        evs = list(extra)
        for b in reads:
            if b.w is not None:
                evs.append(b.w)
        for b in writes:
            if b.w is not None:
                evs.append(b.w)
            evs.extend(b.r.values())
        for (s, v) in evs:
            if eng == "pe" and id(s) in self.pe_sems:
                continue
            if wd.get(id(s), 0) >= v:
                continue
            wd[id(s)] = v
            e.wait_ge(s.h, v)
            self.nins += 1

    def op(self, eng, fn, reads=(), writes=(), mark=True):
        self._waits(eng, reads, writes)
        ins = fn(self.engs[eng])
        self.nins += 1
        if mark:
            s = self.esem[eng]
            if s.n >= 30000:
                s = self.esem[eng] = self.new_sem()
                if eng == "pe":
                    self.pe_sems.add(id(s))
            s.n += 1
            ins.then_inc(s.h, 1)
            ev = (s, s.n)
            for b in reads:
                b.r[id(s)] = ev
            for b in writes:
                b.w = ev
                b.r = {}
        return ins

    def dma(self, q, out, in_, reads=(), writes=(), after=()):
        lst = self.dsems[q]
        i = self.didx[q] % len(lst)
        self.didx[q] += 1
        s = lst[i]
        if s.n >= 30000:
            s = lst[i] = self.new_sem()
        extra = ([(s, s.n)] if s.n > 0 else []) + list(after)
        self._waits(q, reads, writes, extra)
        ins = self.engs[q].dma_start(out=out, in_=in_)
        self.nins += 1
        s.n += 16
        ins.then_inc(s.h, 16)
        ev = (s, s.n)
        for b in reads:
            b.r[id(s)] = ev
        for b in writes:
            b.w = ev
            b.r = {}

    def barrier(self, engines=None):
        for eng in (engines or self.engs):
            e = self.engs[eng]
            wd = self.waited[eng]
            for s in self.all_sems:
                if s.n > 0 and wd.get(id(s), 0) < s.n:
                    if eng == "pe" and id(s) in self.pe_sems:
                        continue
                    wd[id(s)] = s.n
                    e.wait_ge(s.h, s.n)
                    self.nins += 1

    def mm_group(self, out_ap, pairs, reads, writes):
        n = len(pairs)
        for i, (l, r) in enumerate(pairs):
            self.op("pe", (lambda e, l=l, r=r, i=i: e.matmul(out_ap, l, r, start=(i == 0), stop=(i == n - 1))),
                    reads=reads, writes=writes, mark=(i == n - 1))


def run_pipeline(steps, pf=2):
    n = len(steps)
    for i in range(min(pf, n)):
        steps[i][0]()
    for i in range(n):
        if i + pf < n:
            steps[i + pf][0]()
        steps[i][1]()


class Slot:
    def __init__(self, t):
        self.t = t
        self.b = Buf()


def build(n_layers=L, debug=False):
    nc = bass.Bass("TRN2", target_bir_lowering=False)
    skind = "ExternalOutput" if debug else "Internal"

    def din(name, shape, dt):
        return nc.dram_tensor(name, list(shape), dt, kind="ExternalInput").ap()

    def dscr(name, shape, dt, dbg=False):
        return nc.dram_tensor(name, list(shape), dt, kind=(skind if dbg else "Internal")).ap()

    xT = din("xT", [D, S], F32)
    pT = din("pT", [L * PLE, S], F32)
    WSH = {
        "w_in": (6, 16 * 512), "w_out": (4, 16 * 512), "w_gu": (22, 16 * 512), "w_down": (16, 11 * 512),
        "w_pg": (4, 16 * 512), "w_ple": (4, 2 * 512), "w_pool": (1, 512), "w_four": (1, 512),
    }
    wf32 = {n: din(n, [L * t * 128, c], F32) for n, (t, c) in WSH.items()}
    wbf = {n: dscr(n + "_b", [L * t * 128, c], BF16) for n, (t, c) in WSH.items()}
    params_d = din("params", [128, L * NPC], F32)
    dft_d = din("dft", [4 * 4 * 128, 16 * 512], BF16)
    ccsc_d = din("ccsc", [128, 256], BF16)
    ident_d = din("ident", [128, 128], F32)
    ijc_d = din("ijc", [128, 264], BF16)
    outT = nc.dram_tensor("outT", [D, S], F32, kind="ExternalOutput").ap()

    xTb = dscr("xTb", [D, S], BF16)
    xTf = dscr("xTf", [D, S], F32, dbg=True)
    pTb = dscr("pTb", [L * PLE, S], BF16)
    uT = dscr("uT", [1024, S], BF16, dbg=True)
    zpT = dscr("zpT", [512, S], F32, dbg=True)
    ufT = dscr("ufT", [512, S], BF16, dbg=True)
    yT = dscr("yT", [1024, S], F32, dbg=True)
    mixT = dscr("mixT", [D, S], BF16, dbg=True)

    dbuf = {}

    def db(name, c, j):
        k = (name, c, j)
        if k not in dbuf:
            dbuf[k] = Buf()
        return dbuf[k]

    def dbs(name, cs, js):
        return [db(name, c, j) for c in cs for j in js]

    ALLJ = range(NB)

    with ExitStack() as stack:
        K = KB(nc, stack)
        sb = lambda name, shape, dt: stack.enter_context(nc.sbuf_tensor(name, list(shape), dt))
        params = sb("params_s", [128, L * NPC], F32)
        ones = sb("ones_s", [128, 128], F32)
        ccsc = sb("ccsc_s", [128, 256], BF16)
        epst = sb("eps_s", [128, 8], F32)
        ident = sb("ident_s", [128, 128], F32)
        ijc = sb("ijc_s", [128, 264], BF16)
        ps = stack.enter_context(nc.psum_tensor("ps", [128, 8, 512], F32))
        banks = [Slot(None) for _ in range(8)]
        bank_ap = lambda i: ps[:, i, :]
        cbuf = Buf()
        wt = []
        xin = []
        uid = [0]

        def alloc_stream(ph, n_wt, n_xin):
            uid[0] += 1
            wt[:] = [Slot(ph.enter_context(nc.sbuf_tensor(f"wt{uid[0]}_{i}", [128, 16 * 512], BF16))) for i in range(n_wt)]
            xin[:] = [Slot(ph.enter_context(nc.sbuf_tensor(f"xin{uid[0]}_{i}", [128, 16, 512], BF16))) for i in range(n_xin)]

        K.dma("sp", params[:], params_d, writes=[cbuf])
        K.dma("sp", ccsc[:], ccsc_d, writes=[cbuf])
        K.dma("sp", ident[:], ident_d, writes=[cbuf])
        K.dma("sp", ijc[:], ijc_d, writes=[cbuf])
        K.op("dve", lambda e: e.memset(ones[:], 1.0), writes=[cbuf])
        K.op("dve", lambda e: e.memset(epst[:], EPS), writes=[cbuf])
        K.barrier()

        def pcol(l, name, i):
            c = l * NPC + PC[name] + i
            return params[:, c:c + 1]

        cast_jobs = []
        cast_ptr = [0]
        cast_pos = {}

        def add_cast(label, dst, src, bufs_):
            cast_jobs.append((dst, src, bufs_))
            cast_pos[label] = len(cast_jobs)

        def pump(n, after_eng=None):
            for _ in range(n):
                if cast_ptr[0] >= len(cast_jobs):
                    return
                dst, src, bufs_ = cast_jobs[cast_ptr[0]]
                cast_ptr[0] += 1
                after = []
                if after_eng is not None and K.esem[after_eng].n > 0:
                    after = [(K.esem[after_eng], K.esem[after_eng].n)]
                K.dma("pool", dst, src, writes=bufs_, after=after)

        def ensure_cast(label):
            n = cast_pos[label] - cast_ptr[0]
            if n > 0:
                pump(n)

        def cast_weight(name, l):
            t, c = WSH[name]
            for i in range(t):
                r0 = (l * t + i) * 128
                add_cast((name, l), wbf[name][r0:r0 + 128, :], wf32[name][r0:r0 + 128, :], [db(name, l, i)])

        for c in range(16):
            add_cast(("x", 0), xTb[c * 128:(c + 1) * 128, :], xT[c * 128:(c + 1) * 128, :], dbs("xTb", [c], ALLJ))
        for l in range(n_layers):
            for name in ("w_in", "w_pool", "w_four", "w_out"):
                cast_weight(name, l)
            for c in range(2):
                r0 = l * PLE + c * 128
                add_cast(("pT", l), pTb[r0:r0 + 128, :], pT[r0:r0 + 128, :], dbs("pTb", [(l, c)], ALLJ))
            for name in ("w_gu", "w_pg", "w_ple", "w_down"):
                cast_weight(name, l)
        ensure_cast(("pT", 0))

        wt_ctr = [0]

        def next_wt():
            s = wt[wt_ctr[0] % len(wt)]
            wt_ctr[0] += 1
            return s

        bank_ctr = [0]

        def next_bank(pool=(0, 1, 2, 3, 4, 5, 6, 7)):
            i = pool[bank_ctr[0] % len(pool)]
            bank_ctr[0] += 1
            return i

        def load_wtile(slot, name, l, i, ncols):
            t, c = WSH[name]
            r0 = (l * t + i) * 128
            K.dma("sp", slot.t[:, 0:ncols], wbf[name][r0:r0 + 128, :], reads=[db(name, l, i)], writes=[slot.b])

        def load_xblock(slot, src, name, j):
            rd = dbs(name, range(16), [j])
            if name == "mixT":
                rd = rd + dbs("mixT_c0", range(12, 16), [j])
            K.dma("sp", slot.t[:], src.rearrange("(k p) t -> p k t", p=128)[:, :, j * TB:(j + 1) * TB],
                  reads=rd, writes=[slot.b])

        def layer_norm_block(l, j, hb, hbb, gname, bname, s1, s2, tiles, dst_f32, dst_f32_name, last):
            mean, msq, rstd = tiles["mean"][j % 2], tiles["msq"], tiles["rstd"][j % 2]

            def head():
                st1, st2 = next_bank(), next_bank()
                K.op("pe", lambda e: e.matmul(bank_ap(st1), ones[:], s1.t[:], start=True, stop=True),
                     reads=[s1.b, cbuf], writes=[banks[st1].b])
                K.op("pe", lambda e: e.matmul(bank_ap(st2), ones[:], s2.t[:], start=True, stop=True),
                     reads=[s2.b, cbuf], writes=[banks[st2].b])
                K.op("dve", lambda e: e.tensor_scalar(mean.t[:], bank_ap(st1), 1.0 / D, None, ALU.mult),
                     reads=[banks[st1].b], writes=[mean.b])
                K.op("dve", lambda e: e.tensor_tensor(msq.t[:], mean.t[:], mean.t[:], ALU.mult), reads=[mean.b], writes=[msq.b])
                K.op("dve", lambda e: e.scalar_tensor_tensor(rstd.t[:], bank_ap(st2), 1.0 / D, msq.t[:], ALU.mult, ALU.subtract),
                     reads=[banks[st2].b, msq.b], writes=[rstd.b])
                K.op("act", lambda e: e.activation(out=rstd.t[:], in_=rstd.t[:], func=AF.Sqrt, bias=epst[:, 0:1], scale=1.0),
                     reads=[rstd.b, cbuf], writes=[rstd.b])
                K.op("dve", lambda e: e.reciprocal(rstd.t[:], rstd.t[:]), reads=[rstd.b], writes=[rstd.b])

            def chunk(dc):
                of = tiles["of"][dc % 2]
                ob = tiles["ob"][dc % 2]
                hv = hb.t[:, dc, :]
                hbf = hbb[dc]
                K.op("dve", lambda e: e.tensor_tensor(hv, hv, mean.t[:], ALU.subtract), reads=[hbf, mean.b], writes=[hbf])
                K.op("dve", lambda e: e.tensor_tensor(hv, hv, rstd.t[:], ALU.mult), reads=[hbf, rstd.b], writes=[hbf])
                K.op("act", lambda e: e.activation(out=of.t[:], in_=hv, func=AF.Identity,
                                                   bias=pcol(l, bname, dc), scale=pcol(l, gname, dc)),
                     reads=[hbf, cbuf], writes=[of.b])
                K.dma("act", dst_f32[dc * 128:(dc + 1) * 128, j * TB:(j + 1) * TB], of.t[:], reads=[of.b],
                      writes=[db(dst_f32_name, dc, j)])
                if not last:
                    K.op("act", lambda e: e.activation(out=ob.t[:], in_=hv, func=AF.Identity,
                                                       bias=pcol(l, bname, dc), scale=pcol(l, gname, dc)),
                         reads=[hbf, cbuf], writes=[ob.b])
                    K.dma("act", xTb[dc * 128:(dc + 1) * 128, j * TB:(j + 1) * TB], ob.t[:], reads=[ob.b],
                          writes=[db("xTb", dc, j)])

            return [head] + [(lambda dc=dc: chunk(dc)) for dc in range(16)]

        pending = []

        def run_pending(n):
            for _ in range(n):
                if pending:
                    pending.pop(0)()

        def stats_chunk(hb, hbb, dc, hsq, s1, s2, sq_on_pool=False):
            hv = hb.t[:, dc, :]
            if sq_on_pool:
                tgt = s2 if dc == 0 else hsq
                K.op("pool", lambda e: e.tensor_tensor(tgt.t[:], hv, hv, ALU.mult), reads=[hbb[dc]], writes=[tgt.b])
            elif dc == 0:
                K.op("act", lambda e: e.activation(out=s2.t[:], in_=hv, func=AF.Square), reads=[hbb[dc]], writes=[s2.b])
            else:
                K.op("act", lambda e: e.activation(out=hsq.t[:], in_=hv, func=AF.Square), reads=[hbb[dc]], writes=[hsq.b])
            if dc == 0:
                K.op("dve", lambda e: e.tensor_copy(s1.t[:], hv), reads=[hbb[dc]], writes=[s1.b])
            else:
                K.op("dve", lambda e: e.tensor_tensor(s1.t[:], s1.t[:], hv, ALU.add), reads=[hbb[dc], s1.b], writes=[s1.b])
                K.op("pool", lambda e: e.tensor_tensor(s2.t[:], s2.t[:], hsq.t[:], ALU.add), reads=[hsq.b, s2.b], writes=[s2.b])

        for l in range(n_layers):
            last_layer = (l == n_layers - 1)
            xres_src, xres_name = (xT, None) if l == 0 else (xTf, "xTf")

            ensure_cast(("w_in", l))
            with ExitStack() as ph:
                pb = lambda name, shape, dt: ph.enter_context(nc.sbuf_tensor(f"{name}_L{l}", list(shape), dt))
                alloc_stream(ph, 3, 2)
                sig = [Slot(pb(f"m1sig{i}", [128, TB], F32)) for i in range(2)]
                uo = [Slot(pb(f"m1u{i}", [128, TB], F32)) for i in range(4)]
                ub = [Slot(pb(f"m1ub{i}", [128, TB], BF16)) for i in range(4)]
                zb = [Slot(pb(f"m1zb{i}", [128, TB], BF16)) for i in range(2)]
                steps = []
                ctr = [0]
                steps = [(j, eb) for j in range(NB) for eb in range(6)]
                state = {}

                def m1_load(idx):
                    j, eb = steps[idx]
                    if eb == 0:
                        load_xblock(xin[j % len(xin)], xTb, "xTb", j)
                    w = next_wt()
                    load_wtile(w, "w_in", l, eb, 8192)
                    state[idx] = w

                def m1_compute(idx):
                    j, eb = steps[idx]
                    w = state.pop(idx)
                    xs = xin[j % len(xin)]
                    w3 = w.t[:, :].rearrange("p (k n) -> p k n", n=512)
                    bk = []
                    for ci in range(4):
                        bi = next_bank()
                        bk.append(bi)
                        K.mm_group(bank_ap(bi), [(w3[:, kc, ci * 128:(ci + 1) * 128], xs.t[:, kc, :]) for kc in range(16)],
                                   reads=[w.b, xs.b], writes=[banks[bi].b])
                    tsl = slice(j * TB, (j + 1) * TB)
                    if l == 0 and idx % 2 == 0:
                        pump(1, "pe")
                    if eb < 4:
                        for half in range(2):
                            c = eb * 2 + half
                            bv, bg = bk[2 * half], bk[2 * half + 1]
                            sg = sig[ctr[0] % 2]
                            u = ub[ctr[0] % 4]
                            ctr[0] += 1
                            K.op("act", lambda e, sg=sg, bg=bg, c=c: e.activation(out=sg.t[:], in_=bank_ap(bg), func=AF.Sigmoid,
                                                                                 bias=pcol(l, "b_in", 2 * c + 1), scale=1.0),
                                 reads=[banks[bg].b, cbuf], writes=[sg.b])
                            K.op("dve", lambda e, sg=sg, u=u, bv=bv, c=c: e.scalar_tensor_tensor(
                                u.t[:], bank_ap(bv), pcol(l, "b_in", 2 * c), sg.t[:], ALU.add, ALU.mult),
                                reads=[banks[bv].b, sg.b, cbuf], writes=[u.b])
                            K.dma("pool", uT[c * 128:(c + 1) * 128, tsl], u.t[:], reads=[u.b], writes=[db("uT", c, j)])
                    elif eb == 4:
                        for ci in range(4):
                            u = uo[ctr[0] % 4]
                            ctr[0] += 1
                            K.op("act", lambda e, u=u, bi=bk[ci], ci=ci: e.activation(out=u.t[:], in_=bank_ap(bi), func=AF.Identity,
                                                                                     bias=pcol(l, "b_in", 16 + ci), scale=1.0),
                                 reads=[banks[bk[ci]].b, cbuf], writes=[u.b])
                            K.dma("act", zpT[ci * 128:(ci + 1) * 128, tsl], u.t[:], reads=[u.b], writes=[db("zpT", ci, j)])
                    else:
                        for ci in range(4):
                            z = zb[ctr[0] % 2]
                            ctr[0] += 1
                            K.op("act", lambda e, z=z, bi=bk[ci], ci=ci: e.activation(out=z.t[:], in_=bank_ap(bi), func=AF.Identity,
                                                                                     bias=pcol(l, "b_in", 20 + ci), scale=1.0),
                                 reads=[banks[bk[ci]].b, cbuf], writes=[z.b])
                            K.dma("act", ufT[ci * 128:(ci + 1) * 128, tsl], z.t[:], reads=[z.b], writes=[db("ufT", ci, j)])

                run_pipeline([((lambda i=i: m1_load(i)), (lambda i=i: m1_compute(i))) for i in range(len(steps))])
                K.barrier()

            ensure_cast(("w_four", l))
            phs = ExitStack()
            ssum = Slot(phs.enter_context(nc.sbuf_tensor(f"c_ssum_L{l}", [128, S], F32)))
            ssq = Slot(phs.enter_context(nc.sbuf_tensor(f"c_ssq_L{l}", [128, S], F32)))
            with ExitStack() as ph:
                pb = lambda name, shape, dt: ph.enter_context(nc.sbuf_tensor(f"{name}_L{l}", list(shape), dt))
                PADW = S + 30
                up = [Slot(pb(f"c_up{i}", [128, PADW], BF16)) for i in range(2)]
                dg = [Slot(pb(f"c_dg{i}", [128, KW, 128], BF16)) for i in range(2)]
                acc = [Slot(pb(f"c_acc{i}", [128, S], F32)) for i in range(2)]
                sqt = [Slot(pb(f"c_sqt{i}", [128, TB], F32)) for i in range(2)]
                ssum_b = [Buf() for _ in range(NB)]
                ssq_b = [Buf() for _ in range(NB)]
                for s_ in up:
                    K.op("pool", lambda e, s_=s_: e.memset(s_.t[:, 0:15], 0.0), writes=[s_.b])
                    K.op("pool", lambda e, s_=s_: e.memset(s_.t[:, 15 + S:PADW], 0.0), writes=[s_.b])

                def m2_load(c):
                    K.dma("sp", up[c % 2].t[:, 15:15 + S], uT[c * 128:(c + 1) * 128, :], reads=dbs("uT", [c], ALLJ),
                          writes=[up[c % 2].b])
                    gl = dg[c % 2]
                    for jt in range(KW):
                        K.op("dve", lambda e, jt=jt: e.tensor_scalar(gl.t[:, jt, :], ident[:], pcol(l, "wdw", jt * 8 + c), None, ALU.mult),
                             reads=[cbuf], writes=[gl.b], mark=(jt == KW - 1))

                def m2_compute(c):
                    u = up[c % 2]
                    a = acc[c % 2]
                    g_ = dg[c % 2]
                    pump(2, "pe")
                    for tt in range(NB):
                        bi = next_bank()
                        K.mm_group(bank_ap(bi), [(g_.t[:, jt, :], u.t[:, tt * TB + jt:tt * TB + jt + TB]) for jt in range(KW)],
                                   reads=[g_.b, u.b], writes=[banks[bi].b])
                        tsl = slice(tt * TB, (tt + 1) * TB)
                        K.op("act", lambda e, bi=bi, tsl=tsl: e.activation(out=a.t[:, tsl], in_=bank_ap(bi), func=AF.Identity,
                                                                         bias=pcol(l, "b_dw", c), scale=1.0),
                             reads=[banks[bi].b, cbuf], writes=[a.b])
                        if c == 0:
                            K.op("act", lambda e, tsl=tsl: e.activation(out=ssq.t[:, tsl], in_=a.t[:, tsl], func=AF.Square),
                                 reads=[a.b], writes=[ssq_b[tt]])
                            K.op("pool", lambda e, tsl=tsl: e.tensor_copy(ssum.t[:, tsl], a.t[:, tsl]), reads=[a.b], writes=[ssum_b[tt]])
                        else:
                            q_ = sqt[tt % 2]
                            K.op("act", lambda e, tsl=tsl, q_=q_: e.activation(out=q_.t[:], in_=a.t[:, tsl], func=AF.Square),
                                 reads=[a.b], writes=[q_.b])
                            K.op("pool", lambda e, tsl=tsl, q_=q_: e.tensor_tensor(ssq.t[:, tsl], ssq.t[:, tsl], q_.t[:], ALU.add),
                                 reads=[q_.b, ssq_b[tt]], writes=[ssq_b[tt]])
                            K.op("pool", lambda e, tsl=tsl: e.tensor_tensor(ssum.t[:, tsl], ssum.t[:, tsl], a.t[:, tsl], ALU.add),
                                 reads=[a.b, ssum_b[tt]], writes=[ssum_b[tt]])
                    K.dma("act", yT[c * 128:(c + 1) * 128, :], a.t[:], reads=[a.b], writes=dbs("yT", [c], ALLJ))

                with ExitStack() as ph4:
                    pb4 = lambda name, shape, dt: ph4.enter_context(nc.sbuf_tensor(f"{name}_L{l}", list(shape), dt))
                    PW = S + 16
                    xp = Slot(pb4("p_x", [128, PW], F32))
                    wk = [Slot(pb4(f"p_w{i}", [128, PW], F32)) for i in range(2)]
                    pbf = [Slot(pb4(f"p_pb{i}", [128, S], BF16)) for i in range(2)]
                    po = [Slot(pb4(f"p_po{i}", [128, S], BF16)) for i in range(2)]
                    wpl = Slot(pb4("p_wpl", [128, 512], BF16))
                    K.dma("sp", wpl.t[:], wbf["w_pool"][l * 128:(l + 1) * 128, :], reads=[db("w_pool", l, 0)], writes=[wpl.b])
                    K.op("pool", lambda e: e.memset(xp.t[:, 0:8], 0.0), writes=[xp.b])
                    K.op("pool", lambda e: e.memset(xp.t[:, 8 + S:PW], 0.0), writes=[xp.b])

                    def m4_load(g):
                        K.dma("sp", xp.t[:, 8:8 + S], zpT[g * 128:(g + 1) * 128, :], reads=dbs("zpT", [g], ALLJ), writes=[xp.b])

                    def m4_dve(g):
                        k = POOL_K[g]
                        src = xp
                        n = PW
                        step = 1
                        i = 0
                        while step < k:
                            dst = wk[i % 2]
                            n2 = n - step
                            K.op("dve", lambda e, src=src, dst=dst, n2=n2, step=step: e.tensor_tensor(
                                dst.t[:, 0:n2], src.t[:, 0:n2], src.t[:, step:step + n2], ALU.add), reads=[src.b], writes=[dst.b])
                            src = dst
                            n = n2
                            step *= 2
                            i += 1
                        off = 8 - k // 2
                        pbuf = pbf[g % 2]
                        K.op("dve", lambda e, src=src, off=off: e.scalar_tensor_tensor(pbuf.t[:], src.t[:, off:off + S], 1.0 / k,
                                                                                      xp.t[:, 8:8 + S], ALU.mult, ALU.subtract),
                             reads=[src.b, xp.b], writes=[pbuf.b])
                        edges = [(t, t + k // 2) for t in range(k // 2)] + [(t, S - t + k // 2) for t in range(S - k // 2 + 1, S)]
                        for (t, cnt) in edges:
                            K.op("dve", lambda e, src=src, off=off, t=t, cnt=cnt: e.scalar_tensor_tensor(
                                pbuf.t[:, t:t + 1], src.t[:, off + t:off + t + 1], 1.0 / cnt, xp.t[:, 8 + t:8 + t + 1],
                                ALU.mult, ALU.subtract), reads=[src.b, xp.b, pbuf.b], writes=[pbuf.b])

                    def m4_mm(g):
                        pbuf = pbf[g % 2]
                        o = po[g % 2]
                        for jt in range(NB):
                            tsl = slice(jt * TB, (jt + 1) * TB)
                            bi = next_bank()
                            K.op("pe", lambda e, bi=bi, tsl=tsl: e.matmul(bank_ap(bi), wpl.t[:, g * 128:(g + 1) * 128], pbuf.t[:, tsl],
                                                                         start=True, stop=True),
                                 reads=[wpl.b, pbuf.b], writes=[banks[bi].b])
                            K.op("dve", lambda e, bi=bi, tsl=tsl: e.tensor_scalar(o.t[:, tsl], bank_ap(bi), pcol(l, "b_pool", g),
                                                                                 pcol(l, "pscale", g), ALU.add, ALU.mult),
                                 reads=[banks[bi].b, cbuf], writes=[o.b])
                        K.dma("pool", mixT[(8 + g) * 128:(9 + g) * 128, :], o.t[:], reads=[o.b], writes=dbs("mixT", [8 + g], ALLJ))

                    def cm_load(c):
                        m2_load(c)
                        if c % 2 == 0:
                            m4_load(c // 2)

                    def cm_compute(c):
                        m2_compute(c)
                        if c % 2 == 0:
                            m4_dve(c // 2)
                        else:
                            m4_mm(c // 2)

                    run_pipeline([((lambda c=c: cm_load(c)), (lambda c=c: cm_compute(c))) for c in range(8)], pf=1)
                    K.barrier()
                for jt in range(NB):
                    tsl = slice(jt * TB, (jt + 1) * TB)
                    b1, b2 = next_bank(), next_bank()
                    mq = sqt[jt % 2]
                    K.op("pe", lambda e, b1=b1, tsl=tsl: e.matmul(bank_ap(b1), ones[:], ssum.t[:, tsl], start=True, stop=True),
                         reads=[ssum_b[jt], cbuf], writes=[banks[b1].b])
                    K.op("pe", lambda e, b2=b2, tsl=tsl: e.matmul(bank_ap(b2), ones[:], ssq.t[:, tsl], start=True, stop=True),
                         reads=[ssq_b[jt], cbuf], writes=[banks[b2].b])
                    K.op("dve", lambda e, b1=b1, tsl=tsl: e.tensor_scalar(ssum.t[:, tsl], bank_ap(b1), 1.0 / 1024, None, ALU.mult),
                         reads=[banks[b1].b], writes=[ssum_b[jt]])
                    K.op("dve", lambda e, tsl=tsl, mq=mq: e.tensor_tensor(mq.t[:], ssum.t[:, tsl], ssum.t[:, tsl], ALU.mult),
                         reads=[ssum_b[jt]], writes=[mq.b])
                    K.op("dve", lambda e, b2=b2, tsl=tsl, mq=mq: e.scalar_tensor_tensor(ssq.t[:, tsl], bank_ap(b2), 1.0 / 1024, mq.t[:],
                                                                                     ALU.mult, ALU.subtract),
                         reads=[banks[b2].b, mq.b], writes=[ssq_b[jt]])
                    K.op("act", lambda e, tsl=tsl: e.activation(out=ssq.t[:, tsl], in_=ssq.t[:, tsl], func=AF.Sqrt,
                                                                 bias=epst[:, 0:1], scale=1.0), reads=[ssq_b[jt], cbuf], writes=[ssq_b[jt]])
                    K.op("dve", lambda e, tsl=tsl: e.reciprocal(ssq.t[:, tsl], ssq.t[:, tsl]), reads=[ssq_b[jt]], writes=[ssq_b[jt]])
                K.barrier()

            with ExitStack() as ph:
                pb = lambda name, shape, dt: ph.enter_context(nc.sbuf_tensor(f"{name}_L{l}", list(shape), dt))
                alloc_stream(ph, 2, 0)
                AB = Slot(pb("f_ab", [128, 32, 4, 256], BF16))
                ufs = [Slot(pb(f"f_uf{i}", [128, S], BF16)) for i in range(1)]
                zf = [Slot(pb(f"f_zf{i}", [128, TB], BF16)) for i in range(4)]
                fo = [Slot(pb(f"f_fo{i}", [128, TB], BF16)) for i in range(4)]
                zr = [Slot(pb(f"f_zr{i}", [128, TB], BF16)) for i in range(4)]
                fr = [Slot(pb(f"f_fr{i}", [128, TB + 8], BF16)) for i in range(4)]
                Esb = [Slot(pb(f"f_esb{i}", [128, TB], F32)) for i in range(4)]
                ZF = [Slot(pb(f"f_zF{i}", [128, TB], BF16)) for i in range(4)]
                ZR = [Slot(pb(f"f_zR{i}", [128, TB], BF16)) for i in range(4)]
                nyq = Slot(pb("f_nyq", [128, TB], BF16))
                zn = Slot(pb("f_zn", [128, 8], BF16))
                fn = Slot(pb("f_fn", [128, 8], BF16))
                wfo = Slot(pb("f_wfo", [128, 512], BF16))
                K.dma("sp", wfo.t[:], wbf["w_four"][l * 128:(l + 1) * 128, :], reads=[db("w_four", l, 0)], writes=[wfo.b])
                ABb = [[Buf() for _ in range(16)] for _ in range(4)]
                acc1 = Slot(pb("f_m3acc", [128, S], F32))
                ob1 = Slot(pb("f_m3ob", [128, S], BF16))

                def m3_load(c):
                    K.dma("sp", acc1.t[:], yT[c * 128:(c + 1) * 128, :], reads=dbs("yT", [c], ALLJ), writes=[acc1.b])

                def m3_compute(c):
                    a = acc1
                    o = ob1
                    K.op("dve", lambda e: e.tensor_tensor(a.t[:], a.t[:], ssum.t[:], ALU.subtract), reads=[a.b] + ssum_b, writes=[a.b])
                    K.op("pool", lambda e: e.tensor_tensor(a.t[:], a.t[:], ssq.t[:], ALU.mult), reads=[a.b] + ssq_b, writes=[a.b])
                    K.op("act", lambda e: e.activation(out=o.t[:], in_=a.t[:], func=AF.Silu, bias=pcol(l, "cb", c),
                                                       scale=pcol(l, "cg", c)), reads=[a.b, cbuf], writes=[o.b])
                    K.dma("act", mixT[c * 128:(c + 1) * 128, :], o.t[:], reads=[o.b], writes=dbs("mixT", [c], ALLJ))

                def m5a_load(h):
                    K.dma("sp", ufs[0].t[:], ufT[h * 128:(h + 1) * 128, :], reads=dbs("ufT", [h], ALLJ), writes=[ufs[0].b])

                def m5a_compute(h):
                    u = ufs[0]
                    for sp_ in range(16):
                        bi = next_bank()
                        for half in range(2):
                            st = 2 * sp_ + half
                            K.op("pe", lambda e, st=st, half=half, bi=bi: e.matmul(ps[:, bi, half * 256:(half + 1) * 256],
                                                                                  u.t[:, st * 128:(st + 1) * 128], ccsc[:],
                                                                                  start=True, stop=True),
                                 reads=[u.b, cbuf], writes=[banks[bi].b], mark=(half == 1))
                        eng = "act" if sp_ % 2 == 0 else "dve"
                        outv = AB.t[:, 2 * sp_:2 * sp_ + 2, h, :]
                        inv = ps[:, bi, :].rearrange("p (a b) -> p a b", b=256)
                        if eng == "act":
                            K.op("act", lambda e, outv=outv, inv=inv: e.activation(out=outv, in_=inv, func=AF.Copy),
                                 reads=[banks[bi].b], writes=[ABb[h][sp_]])
                        else:
                            K.op("dve", lambda e, outv=outv, inv=inv: e.tensor_copy(outv, inv), reads=[banks[bi].b], writes=[ABb[h][sp_]])

                run_pipeline([((lambda h=h: m5a_load(h)), (lambda h=h: m5a_compute(h))) for h in range(4)], pf=0)

                ABall = [b_ for h_ in range(4) for b_ in ABb[h_]]
                Imat = ijc[:, 0:128]
                Jmat = ijc[:, 128:256]
                bn = next_bank()
                for st in range(32):
                    K.op("pe", lambda e, st=st: e.matmul(ps[0:1, bn, :].rearrange("p (a b) -> p a b", b=128), ijc[:, 256:257], AB.t[:, st, :, 0:128],
                                                         start=(st == 0), stop=(st == 31)),
                         reads=ABall + [cbuf], writes=[banks[bn].b], mark=(st == 31))
                K.op("act", lambda e: e.activation(out=nyq.t[0:1, :], in_=ps[0:1, bn, :], func=AF.Copy), reads=[banks[bn].b], writes=[nyq.b])
                for h in range(4):
                    b1 = next_bank()
                    K.op("pe", lambda e, h=h, b1=b1: e.matmul(ps[:, b1, 0:1], nyq.t[0:1, h * 128:(h + 1) * 128], ijc[0:1, 0:1],
                                                              start=True, stop=True), reads=[nyq.b, cbuf], writes=[banks[b1].b])
                    K.op("act", lambda e, h=h, b1=b1: e.activation(out=zn.t[:, h:h + 1], in_=ps[:, b1, 0:1], func=AF.Copy),
                         reads=[banks[b1].b], writes=[zn.b])
                    b2 = next_bank()
                    K.op("pe", lambda e, h=h, b2=b2: e.matmul(ps[:, b2, 0:1], wfo.t[:, h * 128:(h + 1) * 128], zn.t[:, h:h + 1],
                                                              start=True, stop=True), reads=[wfo.b, zn.b], writes=[banks[b2].b])
                    K.op("act", lambda e, h=h, b2=b2: e.activation(out=fn.t[:, h:h + 1], in_=ps[:, b2, 0:1], func=AF.Identity,
                                                                  bias=pcol(l, "b_f", h), scale=1.0),
                         reads=[banks[b2].b, cbuf], writes=[fn.b])

                st5 = {}
                bsteps = [(kt, part) for kt in range(4) for part in range(4)]
                EBK = (0, 1, 2, 3)
                OBK = (4, 5, 6, 7)
                pctr = [0]

                def pbank():
                    i = OBK[pctr[0] % 4]
                    pctr[0] += 1
                    return i

                def m5b_load(idx):
                    kt, part = bsteps[idx]
                    w = next_wt()
                    r0 = (kt * 4 + part) * 128
                    K.dma("sp", w.t[:, :], dft_d[r0:r0 + 128, :], writes=[w.b])
                    st5[idx] = w
                    if idx % 2 == 0:
                        m3_load(idx // 2)

                def post(kt):
                    for h in range(4):
                        for (Z, mat, zt, ot, rev) in ((ZF, Imat, zf[h], fo[h], False), (ZR, Jmat, zr[h], fr[h], True)):
                            bt = pbank()
                            for kq in range(4):
                                col = (3 - kq) if rev else kq
                                K.op("pe", lambda e, Z=Z, mat=mat, kq=kq, col=col, bt=bt, h=h: e.matmul(
                                    ps[:, bt, col * 128:(col + 1) * 128], Z[kq].t[:, h * 128:(h + 1) * 128], mat, start=True, stop=True),
                                    reads=[Z[kq].b, cbuf], writes=[banks[bt].b], mark=(kq == 3))
                            K.op("act", lambda e, zt=zt, bt=bt: e.activation(out=zt.t[:], in_=bank_ap(bt), func=AF.Copy),
                                 reads=[banks[bt].b], writes=[zt.b])
                            b2 = pbank()
                            K.op("pe", lambda e, zt=zt, b2=b2, h=h: e.matmul(bank_ap(b2), wfo.t[:, h * 128:(h + 1) * 128], zt.t[:],
                                                                            start=True, stop=True),
                                 reads=[wfo.b, zt.b], writes=[banks[b2].b])
                            rows = slice((12 + h) * 128, (13 + h) * 128)
                            if not rev:
                                K.op("act", lambda e, ot=ot, b2=b2, h=h: e.activation(out=ot.t[:], in_=bank_ap(b2), func=AF.Identity,
                                                                                    bias=pcol(l, "b_f", h), scale=1.0),
                                     reads=[banks[b2].b, cbuf], writes=[ot.b])
                                K.dma("act", mixT[rows, kt * TB:(kt + 1) * TB], ot.t[:], reads=[ot.b], writes=[db("mixT", 12 + h, kt)])
                            else:
                                K.op("act", lambda e, ot=ot, b2=b2, h=h: e.activation(out=ot.t[:, 1:1 + TB], in_=bank_ap(b2), func=AF.Identity,
                                                                                    bias=pcol(l, "b_f", h), scale=1.0),
                                     reads=[banks[b2].b, cbuf], writes=[ot.b])
                                c0 = (7 - kt) * TB + 1
                                if kt == 3:
                                    K.op("act", lambda e, ot=ot, h=h: e.activation(out=ot.t[:, 0:1], in_=fn.t[:, h:h + 1], func=AF.Copy),
                                         reads=[fn.b, ot.b], writes=[ot.b])
                                    K.dma("act", mixT[rows, c0 - 1:c0 + TB], ot.t[:, 0:TB + 1], reads=[ot.b],
                                          writes=[db("mixT", 12 + h, 4), db("mixT_c0", 12 + h, 5)])
                                elif kt == 0:
                                    K.dma("act", mixT[rows, c0:c0 + TB - 1], ot.t[:, 1:TB], reads=[ot.b], writes=[db("mixT", 12 + h, 7)])
                                else:
                                    K.dma("act", mixT[rows, c0:c0 + TB], ot.t[:, 1:1 + TB], reads=[ot.b],
                                          writes=[db("mixT", 12 + h, 7 - kt), db("mixT_c0", 12 + h, 8 - kt)])

                def m5b_compute(idx):
                    kt, part = bsteps[idx]
                    w = st5.pop(idx)
                    w3 = w.t[:, :].rearrange("p (k n) -> p k n", n=512)
                    ab, half = part // 2, part % 2
                    if idx % 2 == 0:
                        m3_compute(idx // 2)
                    pump(1, "pe")
                    if part == 1 and kt > 0:
                        post(kt - 1)
                    bks = EBK if ab == 0 else OBK
                    for kq in range(4):
                        bi = bks[kq]
                        for s16 in range(16):
                            st = half * 16 + s16
                            K.op("pe", lambda e, bi=bi, st=st, kq=kq, s16=s16: e.matmul(
                                ps[:, bi, :].rearrange("p (a b) -> p a b", b=128), w3[:, s16, kq * 128:(kq + 1) * 128],
                                AB.t[:, st, :, ab * 128:(ab + 1) * 128],
                                start=(half == 0 and s16 == 0), stop=(half == 1 and s16 == 15)),
                                reads=[w.b] + ABall, writes=[banks[bi].b], mark=(s16 == 15))
                    if part == 1:
                        for kq in range(4):
                            K.op("act", lambda e, kq=kq: e.activation(out=Esb[kq].t[:], in_=bank_ap(EBK[kq]), func=AF.Copy),
                                 reads=[banks[EBK[kq]].b], writes=[Esb[kq].b])
                    if part == 3:
                        for kq in range(4):
                            K.op("dve", lambda e, kq=kq: e.tensor_tensor(ZF[kq].t[:], Esb[kq].t[:], bank_ap(OBK[kq]), ALU.add),
                                 reads=[Esb[kq].b, banks[OBK[kq]].b], writes=[ZF[kq].b])
                            K.op("dve", lambda e, kq=kq: e.tensor_tensor(ZR[kq].t[:], Esb[kq].t[:], bank_ap(OBK[kq]), ALU.subtract),
                                 reads=[Esb[kq].b, banks[OBK[kq]].b], writes=[ZR[kq].b])

                run_pipeline([((lambda i=i: m5b_load(i)), (lambda i=i: m5b_compute(i))) for i in range(len(bsteps))], pf=1)
                post(3)
                K.barrier()
            phs.close()

            ensure_cast(("w_out", l))
            with ExitStack() as ph:
                pb = lambda name, shape, dt: ph.enter_context(nc.sbuf_tensor(f"{name}_L{l}", list(shape), dt))
                alloc_stream(ph, 3, 2)
                hb = [Slot(pb(f"o_hb{i}", [128, 16, TB], F32)) for i in range(2)]
                hbb = [[Buf() for _ in range(16)] for _ in range(2)]
                s1 = [Slot(pb(f"o_s1{i}", [128, TB], F32)) for i in range(2)]
                s2 = [Slot(pb(f"o_s2{i}", [128, TB], F32)) for i in range(2)]
                xr = [Slot(pb(f"o_xr{i}", [128, 4, TB], F32)) for i in range(3)]
                hsq = [Slot(pb(f"o_hsq{i}", [128, TB], F32)) for i in range(2)]
                tiles = {"mean": [Slot(pb(f"o_mean{i}", [128, TB], F32)) for i in range(2)], "msq": Slot(pb("o_msq", [128, TB], F32)),
                         "rstd": [Slot(pb(f"o_rstd{i}", [128, TB], F32)) for i in range(2)],
                         "of": [Slot(pb(f"o_of{i}", [128, TB], F32)) for i in range(2)],
                         "ob": [Slot(pb(f"o_ob{i}", [128, TB], BF16)) for i in range(2)]}
                osteps = [(j, eb) for j in range(NB) for eb in range(4)]
                sto = {}

                def o_load(idx):
                    j, eb = osteps[idx]
                    if eb == 0:
                        load_xblock(xin[j % len(xin)], mixT, "mixT", j)
                    w = next_wt()
                    load_wtile(w, "w_out", l, eb, 8192)
                    x_ = xr[idx % 3]
                    rd = [] if xres_name is None else dbs(xres_name, range(eb * 4, eb * 4 + 4), [j])
                    K.dma("sp", x_.t[:], xres_src[eb * 512:(eb + 1) * 512, j * TB:(j + 1) * TB].rearrange("(c p) t -> p c t", p=128),
                          reads=rd, writes=[x_.b])
                    sto[idx] = (w, x_)

                def o_compute(idx):
                    j, eb = osteps[idx]
                    w, x_ = sto.pop(idx)
                    xs = xin[j % len(xin)]
                    h_ = hb[j % 2]
                    w3 = w.t[:, :].rearrange("p (k n) -> p k n", n=512)
                    bk = []
                    for ci in range(4):
                        bi = next_bank()
                        bk.append(bi)
                        K.mm_group(bank_ap(bi), [(w3[:, kc, ci * 128:(ci + 1) * 128], xs.t[:, kc, :]) for kc in range(16)],
                                   reads=[w.b, xs.b], writes=[banks[bi].b])
                    hbj = hbb[j % 2]
                    for ci in range(4):
                        dc = eb * 4 + ci
                        hv = h_.t[:, dc, :]
                        K.op("dve", lambda e, hv=hv, bi=bk[ci], ci=ci: e.scalar_tensor_tensor(hv, x_.t[:, ci, :], ALPHA, bank_ap(bi),
                                                                                             ALU.mult, ALU.add),
                             reads=[x_.b, banks[bk[ci]].b], writes=[hbj[dc]])
                        K.op("dve", lambda e, hv=hv, dc=dc: e.tensor_scalar(hv, hv, pcol(l, "b_out", dc), None, ALU.add),
                             reads=[hbj[dc], cbuf], writes=[hbj[dc]])
                    run_pending(5)
                    for ci in range(4):
                        dc = eb * 4 + ci
                        stats_chunk(h_, hbj, dc, hsq[dc % 2], s1[j % 2], s2[j % 2], sq_on_pool=True)
                    if eb == 3:
                        pending.extend(layer_norm_block(l, j, h_, hbj, "g1", "b1", s1[j % 2], s2[j % 2], tiles, xTf, "xTf", False))

                run_pipeline([((lambda i=i: o_load(i)), (lambda i=i: o_compute(i))) for i in range(len(osteps))])
                run_pending(100)
                K.barrier()

            for lab in ("pT", "w_gu", "w_pg", "w_ple", "w_down"):
                ensure_cast((lab, l))
            with ExitStack() as ph:
                pb = lambda name, shape, dt: ph.enter_context(nc.sbuf_tensor(f"{name}_L{l}", list(shape), dt))
                alloc_stream(ph, 3, 1)
                hid = Slot(pb("f_hid", [128, NF, TB], BF16))
                hb = [Slot(pb("f_hb0", [128, 16, TB], F32))]
                hbb = [Buf() for _ in range(16)]
                s1 = Slot(pb("f_s1", [128, TB], F32))
                s2 = Slot(pb("f_s2", [128, TB], F32))
                xr = [Slot(pb(f"f_xr{i}", [128, 4, TB], F32)) for i in range(2)]
                pts = [Slot(pb(f"f_pt{i}", [128, 2, TB], BF16)) for i in range(2)]
                wple = Slot(pb("f_wple", [128, 4, 1024], BF16))
                sg = [Slot(pb(f"f_sg{i}", [128, TB], F32)) for i in range(2)]
                pl = [Slot(pb(f"f_pl{i}", [128, TB], F32)) for i in range(2)]
                hsq = [Slot(pb(f"f_hsq{i}", [128, TB], F32)) for i in range(1)] * 2
                tiles = {"mean": [Slot(pb(f"f_mean{i}", [128, TB], F32)) for i in range(2)], "msq": Slot(pb("f_msq", [128, TB], F32)),
                         "rstd": [Slot(pb(f"f_rstd{i}", [128, TB], F32)) for i in range(2)],
                         "of": [Slot(pb(f"f_of{i}", [128, TB], F32)) for i in range(2)],
                         "ob": [Slot(pb(f"f_ob{i}", [128, TB], BF16)) for i in range(2)]}
                for eb in range(4):
                    r0 = (l * 4 + eb) * 128
                    K.dma("sp", wple.t[:, eb, :], wbf["w_ple"][r0:r0 + 128, :], reads=[db("w_ple", l, eb)], writes=[wple.b])
                fsteps = []
                for j in range(NB):
                    for t in range(22):
                        fsteps.append((j, "gu", t, 0))
                    for eb in range(4):
                        fsteps.append((j, "pg", eb, 0))
                        for kg in range(4):
                            fsteps.append((j, "dn", eb, kg))
                stf = {}
                sgc = [0]
                dst_f32, dst_name = (outT, "outT") if last_layer else (xTf, "xTf")

                def f_load(idx):
                    j, kind, a, kg = fsteps[idx]
                    if kind == "gu" and a == 0:
                        load_xblock(xin[j % len(xin)], xTb, "xTb", j)
                        pt = pts[j % 2]
                        r0 = l * PLE
                        K.dma("sp", pt.t[:], pTb[r0:r0 + PLE, j * TB:(j + 1) * TB].rearrange("(c p) t -> p c t", p=128),
                              reads=dbs("pTb", [(l, 0), (l, 1)], [j]), writes=[pt.b])
                    w = next_wt()
                    x_ = None
                    if kind == "gu":
                        load_wtile(w, "w_gu", l, a, 8192)
                    elif kind == "pg":
                        load_wtile(w, "w_pg", l, a, 8192)
                        x_ = xr[a % 2]
                        K.dma("sp", x_.t[:], xTf[a * 512:(a + 1) * 512, j * TB:(j + 1) * TB].rearrange("(c p) t -> p c t", p=128),
                              reads=dbs("xTf", range(a * 4, a * 4 + 4), [j]), writes=[x_.b])
                    else:
                        load_wtile(w, "w_down", l, a * 4 + kg, 11 * 512)
                    stf[idx] = (w, x_)

                def f_compute(idx):
                    j, kind, a, kg = fsteps[idx]
                    w, x_ = stf.pop(idx)
                    xs = xin[j % len(xin)]
                    h_ = hb[0]
                    if kind == "gu":
                        w3 = w.t[:, :].rearrange("p (k n) -> p k n", n=512)
                        bk = []
                        for ci in range(4):
                            bi = next_bank()
                            bk.append(bi)
                            K.mm_group(bank_ap(bi), [(w3[:, kc, ci * 128:(ci + 1) * 128], xs.t[:, kc, :]) for kc in range(16)],
                                       reads=[w.b, xs.b], writes=[banks[bi].b])
                        for half in range(2):
                            f = 2 * a + half
                            bg, bu = bk[2 * half], bk[2 * half + 1]
                            s_ = sg[sgc[0] % 2]
                            sgc[0] += 1
                            K.op("act", lambda e, s_=s_, bg=bg: e.activation(out=s_.t[:], in_=bank_ap(bg), func=AF.Silu),
                                 reads=[banks[bg].b], writes=[s_.b])
                            K.op("dve", lambda e, s_=s_, bu=bu, f=f: e.tensor_tensor(hid.t[:, f, :], s_.t[:], bank_ap(bu), ALU.mult),
                                 reads=[s_.b, banks[bu].b], writes=[hid.b])
                    elif kind == "pg":
                        eb = a
                        w3 = w.t[:, :].rearrange("p (k n) -> p k n", n=512)
                        pgpool = (4, 5, 6, 7) if eb % 2 == 0 else (0, 1, 2, 3)
                        pt = pts[j % 2]
                        for ci in range(4):
                            dc = eb * 4 + ci
                            bp = next_bank(pgpool)
                            bq = next_bank(pgpool)
                            K.mm_group(bank_ap(bp), [(w3[:, kc, ci * 128:(ci + 1) * 128], xs.t[:, kc, :]) for kc in range(16)],
                                       reads=[w.b, xs.b], writes=[banks[bp].b])
                            K.mm_group(bank_ap(bq), [(wple.t[:, eb, kc * 512 + ci * 128:kc * 512 + (ci + 1) * 128], pt.t[:, kc, :])
                                                     for kc in range(2)], reads=[wple.b, pt.b], writes=[banks[bq].b])
                            s_ = sg[sgc[0] % 2]
                            sgc[0] += 1
                            p_ = pl[dc % 2]
                            K.op("act", lambda e, s_=s_, bp=bp: e.activation(out=s_.t[:], in_=bank_ap(bp), func=AF.Sigmoid),
                                 reads=[banks[bp].b], writes=[s_.b])
                            K.op("dve", lambda e, s_=s_, p_=p_, bq=bq: e.tensor_tensor(p_.t[:], s_.t[:], bank_ap(bq), ALU.mult),
                                 reads=[s_.b, banks[bq].b], writes=[p_.b])
                            hv = h_.t[:, dc, :]
                            K.op("dve", lambda e, hv=hv, p_=p_, ci=ci: e.scalar_tensor_tensor(hv, x_.t[:, ci, :], ALPHA, p_.t[:],
                                                                                            ALU.mult, ALU.add),
                                 reads=[x_.b, p_.b], writes=[hbb[dc]])
                    else:
                        eb = a
                        w3 = w.t[:, 0:11 * 512].rearrange("p (k n) -> p k n", n=512)
                        dpool = (0, 1, 2, 3) if eb % 2 == 0 else (4, 5, 6, 7)
                        for ci in range(4):
                            bi = dpool[ci]
                            for kc in range(11):
                                lastk = (kc == 10)
                                mk = (lastk and (kg == 3 or ci == 3))
                                K.op("pe", lambda e, bi=bi, ci=ci, kc=kc: e.matmul(
                                    bank_ap(bi), w3[:, kc, ci * 128:(ci + 1) * 128], hid.t[:, kg * 11 + kc, :],
                                    start=(kg == 0 and kc == 0), stop=(kg == 3 and kc == 10)),
                                    reads=[w.b, hid.b], writes=[banks[bi].b], mark=mk)
                        if kg == 3:
                            for ci in range(4):
                                dc = eb * 4 + ci
                                bi = dpool[ci]
                                hv = h_.t[:, dc, :]
                                K.op("dve", lambda e, hv=hv, bi=bi: e.tensor_tensor(hv, hv, bank_ap(bi), ALU.add),
                                     reads=[hbb[dc], banks[bi].b], writes=[hbb[dc]])
                                stats_chunk(h_, hbb, dc, hsq[dc % 2], s1, s2)
                            if eb == 3:
                                pending.extend(layer_norm_block(l, j, h_, hbb, "g2", "b2", s1, s2, tiles, dst_f32, dst_name, last_layer))
                    if kind == "gu":
                        run_pending(1)
                    if idx % 4 == 0:
                        pump(1, "pe")

                run_pipeline([((lambda i=i: f_load(i)), (lambda i=i: f_compute(i))) for i in range(len(fsteps))])
                run_pending(100)
                K.barrier()

        K.barrier(["sp"])
        build.nins = K.nins
    return nc


def _block(w, kcg):
    Kd, E = w.shape
    KG = Kd // (128 * kcg)
    EB = E // 512
    return np.ascontiguousarray(w.reshape(KG, kcg, 128, EB, 512).transpose(3, 0, 2, 1, 4)).reshape(EB * KG * 128, kcg * 512)


def _cols(v):
    return np.ascontiguousarray(v.reshape(-1, 128).T)


def _prep_shared(inp, n_layers=L):
    f32 = np.float32
    out = {}
    w_in_l, w_out_l, w_gu_l, w_down_l, w_pg_l, w_ple_l, w_pool_l, w_four_l = [], [], [], [], [], [], [], []
    params = np.zeros((128, L * NPC), f32)
    order = []
    for c in range(8):
        order += [c, 8 + c]
    order += list(range(16, 24))
    for l in range(L):
        w_in = np.asarray(inp["w_in"][l], f32).reshape(D, 24, 128)[:, order, :].reshape(D, 3072)
        w_in_l.append(_block(w_in, 16))
        w_out_l.append(_block(np.asarray(inp["w_out"][l], f32), 16))
        g = np.asarray(inp["w_gate"][l], f32).reshape(D, NF, 128)
        u = np.asarray(inp["w_up"][l], f32).reshape(D, NF, 128)
        gu = np.stack([g, u], axis=2).reshape(D, 2 * DFF)
        w_gu_l.append(_block(gu, 16))
        w_down_l.append(_block(np.asarray(inp["w_down"][l], f32), 11))
        w_pg_l.append(_block(np.asarray(inp["w_ple_gate"][l], f32), 16))
        w_ple_l.append(_block(np.asarray(inp["w_ple"][l], f32), 2))
        w_pool_l.append(np.ascontiguousarray(np.asarray(inp["w_pool"][l], f32).transpose(1, 0, 2)).reshape(128, 512))
        w_four_l.append(np.ascontiguousarray(np.asarray(inp["w_fourier"][l], f32).transpose(1, 0, 2)).reshape(128, 512))
        P = params[:, l * NPC:(l + 1) * NPC]
        b_in = np.asarray(inp["b_in"][l], f32).reshape(24, 128)[order, :].reshape(-1)
        P[:, PC["b_in"]:PC["b_in"] + 24] = _cols(b_in)
        P[:, PC["b_dw"]:PC["b_dw"] + 8] = _cols(np.asarray(inp["b_dw"][l], f32))
        P[:, PC["cg"]:PC["cg"] + 8] = _cols(np.asarray(inp["conv_ln_g"][l], f32))
        P[:, PC["cb"]:PC["cb"] + 8] = _cols(np.asarray(inp["conv_ln_b"][l], f32))
        P[:, PC["b_pool"]:PC["b_pool"] + 4] = _cols(np.asarray(inp["b_pool"][l], f32).reshape(-1))
        P[:, PC["pscale"]:PC["pscale"] + 4] = _cols(np.asarray(inp["pool_scale"][l], f32))
        P[:, PC["b_f"]:PC["b_f"] + 4] = _cols(np.asarray(inp["b_fourier"][l], f32).reshape(-1))
        P[:, PC["b_out"]:PC["b_out"] + 16] = _cols(np.asarray(inp["b_out"][l], f32))
        P[:, PC["g1"]:PC["g1"] + 16] = _cols(np.asarray(inp["ln1_g"][l], f32))
        P[:, PC["b1"]:PC["b1"] + 16] = _cols(np.asarray(inp["ln1_b"][l], f32))
        P[:, PC["g2"]:PC["g2"] + 16] = _cols(np.asarray(inp["ln2_g"][l], f32))
        P[:, PC["b2"]:PC["b2"] + 16] = _cols(np.asarray(inp["ln2_b"][l], f32))
        wdw = np.asarray(inp["w_dw"][l], f32)
        for jt in range(KW):
            P[:, PC["wdw"] + jt * 8:PC["wdw"] + jt * 8 + 8] = _cols(wdw[jt])
    out["w_in"] = np.concatenate(w_in_l, 0)
    out["w_out"] = np.concatenate(w_out_l, 0)
    out["w_gu"] = np.concatenate(w_gu_l, 0)
    out["w_down"] = np.concatenate(w_down_l, 0)
    out["w_pg"] = np.concatenate(w_pg_l, 0)
    out["w_ple"] = np.concatenate(w_ple_l, 0)
    out["w_pool"] = np.concatenate(w_pool_l, 0)
    out["w_four"] = np.concatenate(w_four_l, 0)
    out["params"] = params
    idx = np.arange(S, dtype=np.int64)
    ang = (2.0 * np.pi / S) * ((idx[:, None] * idx[None, :]) % S).astype(np.float64)
    cs = (np.cos(ang) / 64.0)
    ss = (-np.sin(ang) / 64.0)
    dft = np.empty((8, 4, 128, 16, 512), ml_dtypes.bfloat16)
    for part, m in enumerate((cs, cs, ss, ss)):
        half = part % 2
        blk = m[half * 2048:(half + 1) * 2048, :].reshape(16, 128, 8, 512)
        dft[:, part] = blk.transpose(2, 1, 0, 3).astype(ml_dtypes.bfloat16)
    out["dft"] = np.ascontiguousarray(dft[0:4]).reshape(4 * 4 * 128, 16 * 512)
    c = np.arange(128)
    a2 = (2.0 * np.pi / 128) * ((c[:, None] * c[None, :]) % 128)
    sc = 1.0 / np.sqrt(128.0)
    out["ccsc"] = np.concatenate([np.cos(a2) * sc, np.sin(a2) * sc], axis=1).astype(ml_dtypes.bfloat16)
    out["ident"] = np.eye(128, dtype=f32)
    ijc = np.zeros((128, 264), f32)
    ijc[:, 0:128] = np.eye(128)
    ijc[:, 128:256] = np.eye(128)[::-1]
    ijc[:, 256] = ((-1.0) ** np.arange(128)) / 64.0
    out["ijc"] = ijc.astype(ml_dtypes.bfloat16)
    return out


_NC_CACHE = {}


def kernel(**inputs):
    n = 8
    shared = _prep_shared(inputs)
    x = np.asarray(inputs["x"], np.float32)
    p = np.asarray(inputs["p"], np.float32)
    in_maps = []
    for b in range(n):
        m = dict(shared)
        m["xT"] = np.ascontiguousarray(x[b].T)
        m["pT"] = np.ascontiguousarray(p[:, b].transpose(0, 2, 1)).reshape(L * PLE, S)
        in_maps.append(m)
    if "nc" not in _NC_CACHE:
        _NC_CACHE["nc"] = build()
    res = run_bass_kernel_spmd(_NC_CACHE["nc"], in_maps, core_ids=list(range(n)))
    out = np.empty((n, S, D), np.float32)
    for b in range(n):
        out[b] = np.asarray(res.results[b]["outT"]).T
    return out
```

```python
import numpy as np
import ml_dtypes
from contextlib import ExitStack
import concourse.bass as bass
import concourse.mybir as mybir
from concourse.bass_utils import run_bass_kernel_spmd

F32 = mybir.dt.float32
BF16 = mybir.dt.bfloat16
AF = mybir.ActivationFunctionType
ALU = mybir.AluOpType

S = 4096
D = 2048
L = 4
TB = 512
NB = S // TB
DFF = 5632
NF = DFF // 128
PLE = 256
ALPHA = float((2 * L) ** 0.25)
EPS = 1e-5
KW = 31
POOL_K = (2, 4, 8, 16)

PC = {}
_o = 0
for _n, _c in (("b_in", 24), ("b_dw", 8), ("cg", 8), ("cb", 8), ("b_pool", 4), ("pscale", 4), ("b_f", 4),
               ("b_out", 16), ("g1", 16), ("b1", 16), ("g2", 16), ("b2", 16), ("wdw", KW * 8)):
    PC[_n] = _o
    _o += _c
NPC = _o


class Buf:
    __slots__ = ("w", "r")

    def __init__(self):
        self.w = None
        self.r = {}


class Sem:
    __slots__ = ("h", "n")

    def __init__(self, h):
        self.h = h
        self.n = 0


class KB:
    def __init__(self, nc, stack):
        self.nc = nc
        self.stack = stack
        self.all_sems = []
        self.engs = {"pe": nc.tensor, "act": nc.scalar, "dve": nc.vector, "pool": nc.gpsimd, "sp": nc.sync}
        self.esem = {e: self.new_sem() for e in self.engs}
        self.pe_sems = {id(self.esem["pe"])}
        self.waited = {e: {} for e in self.engs}
        self.dsems = {"sp": [self.new_sem() for _ in range(20)], "pool": [self.new_sem() for _ in range(12)],
                      "act": [self.new_sem() for _ in range(16)]}
        self.didx = {q: 0 for q in self.dsems}
        self.nins = 0

    def new_sem(self):
        h = self.stack.enter_context(self.nc.semaphore(f"s{len(self.all_sems)}"))
        s = Sem(h)
        self.all_sems.append(s)
        return s

    def _waits(self, eng, reads, writes, extra=()):
        e = self.engs[eng]
        wd = self.waited[eng]
        evs = list(extra)
        for b in reads:
            if b.w is not None:
                evs.append(b.w)
        for b in writes:
            if b.w is not None:
                evs.append(b.w)
            evs.extend(b.r.values())
        for (s, v) in evs:
            if eng == "pe" and id(s) in self.pe_sems:
                continue
            if wd.get(id(s), 0) >= v:
                continue
            wd[id(s)] = v
            e.wait_ge(s.h, v)
            self.nins += 1

    def op(self, eng, fn, reads=(), writes=(), mark=True):
        self._waits(eng, reads, writes)
        ins = fn(self.engs[eng])
        self.nins += 1
        if mark:
            s = self.esem[eng]
            if s.n >= 30000:
                s = self.esem[eng] = self.new_sem()
                if eng == "pe":
                    self.pe_sems.add(id(s))
            s.n += 1
            ins.then_inc(s.h, 1)
            ev = (s, s.n)
            for b in reads:
                b.r[id(s)] = ev
            for b in writes:
                b.w = ev
                b.r = {}
        return ins

    def dma(self, q, out, in_, reads=(), writes=(), after=()):
        lst = self.dsems[q]
        i = self.didx[q] % len(lst)
        self.didx[q] += 1
        s = lst[i]
        if s.n >= 30000:
            s = lst[i] = self.new_sem()
        extra = ([(s, s.n)] if s.n > 0 else []) + list(after)
        self._waits(q, reads, writes, extra)
        ins = self.engs[q].dma_start(out=out, in_=in_)
        self.nins += 1
        s.n += 16
        ins.then_inc(s.h, 16)
        ev = (s, s.n)
        for b in reads:
            b.r[id(s)] = ev
        for b in writes:
            b.w = ev
            b.r = {}

    def barrier(self, engines=None):
        for eng in (engines or self.engs):
            e = self.engs[eng]
            wd = self.waited[eng]
            for s in self.all_sems:
                if s.n > 0 and wd.get(id(s), 0) < s.n:
                    if eng == "pe" and id(s) in self.pe_sems:
                        continue
                    wd[id(s)] = s.n
                    e.wait_ge(s.h, s.n)
                    self.nins += 1

    def mm_group(self, out_ap, pairs, reads, writes):
        n = len(pairs)
        for i, (l, r) in enumerate(pairs):
            self.op("pe", (lambda e, l=l, r=r, i=i: e.matmul(out_ap, l, r, start=(i == 0), stop=(i == n - 1))),
                    reads=reads, writes=writes, mark=(i == n - 1))


def run_pipeline(steps, pf=2):
    n = len(steps)
    for i in range(min(pf, n)):
        steps[i][0]()
    for i in range(n):
        if i + pf < n:
            steps[i + pf][0]()
        steps[i][1]()


class Slot:
    def __init__(self, t):
        self.t = t
        self.b = Buf()


def build(n_layers=L, debug=False):
    nc = bass.Bass("TRN2", target_bir_lowering=False)
    skind = "ExternalOutput" if debug else "Internal"

    def din(name, shape, dt):
        return nc.dram_tensor(name, list(shape), dt, kind="ExternalInput").ap()

    def dscr(name, shape, dt, dbg=False):
        return nc.dram_tensor(name, list(shape), dt, kind=(skind if dbg else "Internal")).ap()

    xT = din("xT", [D, S], F32)
    pT = din("pT", [L * PLE, S], F32)
    WSH = {
        "w_in": (6, 16 * 512), "w_out": (4, 16 * 512), "w_gu": (22, 16 * 512), "w_down": (16, 11 * 512),
        "w_pg": (4, 16 * 512), "w_ple": (4, 2 * 512), "w_pool": (1, 512), "w_four": (1, 512),
    }
    wf32 = {n: din(n, [L * t * 128, c], F32) for n, (t, c) in WSH.items()}
    wbf = {n: dscr(n + "_b", [L * t * 128, c], BF16) for n, (t, c) in WSH.items()}
    params_d = din("params", [128, L * NPC], F32)
    dft_d = din("dft", [4 * 4 * 128, 16 * 512], BF16)
    ccsc_d = din("ccsc", [128, 256], BF16)
    ident_d = din("ident", [128, 128], F32)
    ijc_d = din("ijc", [128, 264], BF16)
    outT = nc.dram_tensor("outT", [D, S], F32, kind="ExternalOutput").ap()

    xTb = dscr("xTb", [D, S], BF16)
    xTf = dscr("xTf", [D, S], F32, dbg=True)
    pTb = dscr("pTb", [L * PLE, S], BF16)
    uT = dscr("uT", [1024, S], BF16, dbg=True)
    zpT = dscr("zpT", [512, S], F32, dbg=True)
    ufT = dscr("ufT", [512, S], BF16, dbg=True)
    yT = dscr("yT", [1024, S], F32, dbg=True)
    mixT = dscr("mixT", [D, S], BF16, dbg=True)

    dbuf = {}

    def db(name, c, j):
        k = (name, c, j)
        if k not in dbuf:
            dbuf[k] = Buf()
        return dbuf[k]

    def dbs(name, cs, js):
        return [db(name, c, j) for c in cs for j in js]

    ALLJ = range(NB)

    with ExitStack() as stack:
        K = KB(nc, stack)
        sb = lambda name, shape, dt: stack.enter_context(nc.sbuf_tensor(name, list(shape), dt))
        params = sb("params_s", [128, L * NPC], F32)
        ones = sb("ones_s", [128, 128], F32)
        ccsc = sb("ccsc_s", [128, 256], BF16)
        epst = sb("eps_s", [128, 8], F32)
        ident = sb("ident_s", [128, 128], F32)
        ijc = sb("ijc_s", [128, 264], BF16)
        ps = stack.enter_context(nc.psum_tensor("ps", [128, 8, 512], F32))
        banks = [Slot(None) for _ in range(8)]
        bank_ap = lambda i: ps[:, i, :]
        cbuf = Buf()
        wt = []
        xin = []
        uid = [0]

        def alloc_stream(ph, n_wt, n_xin):
            uid[0] += 1
            wt[:] = [Slot(ph.enter_context(nc.sbuf_tensor(f"wt{uid[0]}_{i}", [128, 16 * 512], BF16))) for i in range(n_wt)]
            xin[:] = [Slot(ph.enter_context(nc.sbuf_tensor(f"xin{uid[0]}_{i}", [128, 16, 512], BF16))) for i in range(n_xin)]

        K.dma("sp", params[:], params_d, writes=[cbuf])
        K.dma("sp", ccsc[:], ccsc_d, writes=[cbuf])
        K.dma("sp", ident[:], ident_d, writes=[cbuf])
        K.dma("sp", ijc[:], ijc_d, writes=[cbuf])
        K.op("dve", lambda e: e.memset(ones[:], 1.0), writes=[cbuf])
        K.op("dve", lambda e: e.memset(epst[:], EPS), writes=[cbuf])
        K.barrier()

        def pcol(l, name, i):
            c = l * NPC + PC[name] + i
            return params[:, c:c + 1]

        cast_jobs = []
        cast_ptr = [0]
        cast_pos = {}

        def add_cast(label, dst, src, bufs_):
            cast_jobs.append((dst, src, bufs_))
            cast_pos[label] = len(cast_jobs)

        def pump(n, after_eng=None):
            for _ in range(n):
                if cast_ptr[0] >= len(cast_jobs):
                    return
                dst, src, bufs_ = cast_jobs[cast_ptr[0]]
                cast_ptr[0] += 1
                after = []
                if after_eng is not None and K.esem[after_eng].n > 0:
                    after = [(K.esem[after_eng], K.esem[after_eng].n)]
                K.dma("pool", dst, src, writes=bufs_, after=after)

        def ensure_cast(label):
            n = cast_pos[label] - cast_ptr[0]
            if n > 0:
                pump(n)

        def cast_weight(name, l):
            t, c = WSH[name]
            for i in range(t):
                r0 = (l * t + i) * 128
                add_cast((name, l), wbf[name][r0:r0 + 128, :], wf32[name][r0:r0 + 128, :], [db(name, l, i)])

        for c in range(16):
            add_cast(("x", 0), xTb[c * 128:(c + 1) * 128, :], xT[c * 128:(c + 1) * 128, :], dbs("xTb", [c], ALLJ))
        for l in range(n_layers):
            for name in ("w_in", "w_pool", "w_four", "w_out"):
                cast_weight(name, l)
            for c in range(2):
                r0 = l * PLE + c * 128
                add_cast(("pT", l), pTb[r0:r0 + 128, :], pT[r0:r0 + 128, :], dbs("pTb", [(l, c)], ALLJ))
            for name in ("w_gu", "w_pg", "w_ple", "w_down"):
                cast_weight(name, l)
        ensure_cast(("pT", 0))

        wt_ctr = [0]

        def next_wt():
            s = wt[wt_ctr[0] % len(wt)]
            wt_ctr[0] += 1
            return s

        bank_ctr = [0]

        def next_bank(pool=(0, 1, 2, 3, 4, 5, 6, 7)):
            i = pool[bank_ctr[0] % len(pool)]
            bank_ctr[0] += 1
            return i

        def load_wtile(slot, name, l, i, ncols):
            t, c = WSH[name]
            r0 = (l * t + i) * 128
            K.dma("sp", slot.t[:, 0:ncols], wbf[name][r0:r0 + 128, :], reads=[db(name, l, i)], writes=[slot.b])

        def load_xblock(slot, src, name, j):
            rd = dbs(name, range(16), [j])
            if name == "mixT":
                rd = rd + dbs("mixT_c0", range(12, 16), [j])
            K.dma("sp", slot.t[:], src.rearrange("(k p) t -> p k t", p=128)[:, :, j * TB:(j + 1) * TB],
                  reads=rd, writes=[slot.b])

        def layer_norm_block(l, j, hb, hbb, gname, bname, s1, s2, tiles, dst_f32, dst_f32_name, last):
            mean, msq, rstd = tiles["mean"][j % 2], tiles["msq"], tiles["rstd"][j % 2]

            def head():
                st1, st2 = next_bank(), next_bank()
                K.op("pe", lambda e: e.matmul(bank_ap(st1), ones[:], s1.t[:], start=True, stop=True),
                     reads=[s1.b, cbuf], writes=[banks[st1].b])
                K.op("pe", lambda e: e.matmul(bank_ap(st2), ones[:], s2.t[:], start=True, stop=True),
                     reads=[s2.b, cbuf], writes=[banks[st2].b])
                K.op("dve", lambda e: e.tensor_scalar(mean.t[:], bank_ap(st1), 1.0 / D, None, ALU.mult),
                     reads=[banks[st1].b], writes=[mean.b])
                K.op("dve", lambda e: e.tensor_tensor(msq.t[:], mean.t[:], mean.t[:], ALU.mult), reads=[mean.b], writes=[msq.b])
                K.op("dve", lambda e: e.scalar_tensor_tensor(rstd.t[:], bank_ap(st2), 1.0 / D, msq.t[:], ALU.mult, ALU.subtract),
                     reads=[banks[st2].b, msq.b], writes=[rstd.b])
                K.op("act", lambda e: e.activation(out=rstd.t[:], in_=rstd.t[:], func=AF.Sqrt, bias=epst[:, 0:1], scale=1.0),
                     reads=[rstd.b, cbuf], writes=[rstd.b])
                K.op("dve", lambda e: e.reciprocal(rstd.t[:], rstd.t[:]), reads=[rstd.b], writes=[rstd.b])

            def chunk(dc):
                of = tiles["of"][dc % 2]
                ob = tiles["ob"][dc % 2]
                hv = hb.t[:, dc, :]
                hbf = hbb[dc]
                K.op("dve", lambda e: e.tensor_tensor(hv, hv, mean.t[:], ALU.subtract), reads=[hbf, mean.b], writes=[hbf])
                K.op("dve", lambda e: e.tensor_tensor(hv, hv, rstd.t[:], ALU.mult), reads=[hbf, rstd.b], writes=[hbf])
                K.op("act", lambda e: e.activation(out=of.t[:], in_=hv, func=AF.Identity,
                                                   bias=pcol(l, bname, dc), scale=pcol(l, gname, dc)),
                     reads=[hbf, cbuf], writes=[of.b])
                K.dma("act", dst_f32[dc * 128:(dc + 1) * 128, j * TB:(j + 1) * TB], of.t[:], reads=[of.b],
                      writes=[db(dst_f32_name, dc, j)])
                if not last:
                    K.op("act", lambda e: e.activation(out=ob.t[:], in_=hv, func=AF.Identity,
                                                       bias=pcol(l, bname, dc), scale=pcol(l, gname, dc)),
                         reads=[hbf, cbuf], writes=[ob.b])
                    K.dma("act", xTb[dc * 128:(dc + 1) * 128, j * TB:(j + 1) * TB], ob.t[:], reads=[ob.b],
                          writes=[db("xTb", dc, j)])

            return [head] + [(lambda dc=dc: chunk(dc)) for dc in range(16)]

        pending = []

        def run_pending(n):
            for _ in range(n):
                if pending:
                    pending.pop(0)()

        def stats_chunk(hb, hbb, dc, hsq, s1, s2, sq_on_pool=False):
            hv = hb.t[:, dc, :]
            if sq_on_pool:
                tgt = s2 if dc == 0 else hsq
                K.op("pool", lambda e: e.tensor_tensor(tgt.t[:], hv, hv, ALU.mult), reads=[hbb[dc]], writes=[tgt.b])
            elif dc == 0:
                K.op("act", lambda e: e.activation(out=s2.t[:], in_=hv, func=AF.Square), reads=[hbb[dc]], writes=[s2.b])
            else:
                K.op("act", lambda e: e.activation(out=hsq.t[:], in_=hv, func=AF.Square), reads=[hbb[dc]], writes=[hsq.b])
            if dc == 0:
                K.op("dve", lambda e: e.tensor_copy(s1.t[:], hv), reads=[hbb[dc]], writes=[s1.b])
            else:
                K.op("dve", lambda e: e.tensor_tensor(s1.t[:], s1.t[:], hv, ALU.add), reads=[hbb[dc], s1.b], writes=[s1.b])
                K.op("pool", lambda e: e.tensor_tensor(s2.t[:], s2.t[:], hsq.t[:], ALU.add), reads=[hsq.b, s2.b], writes=[s2.b])

        for l in range(n_layers):
            last_layer = (l == n_layers - 1)
            xres_src, xres_name = (xT, None) if l == 0 else (xTf, "xTf")

            ensure_cast(("w_in", l))
            with ExitStack() as ph:
                pb = lambda name, shape, dt: ph.enter_context(nc.sbuf_tensor(f"{name}_L{l}", list(shape), dt))
                alloc_stream(ph, 3, 2)
                sig = [Slot(pb(f"m1sig{i}", [128, TB], F32)) for i in range(2)]
                uo = [Slot(pb(f"m1u{i}", [128, TB], F32)) for i in range(4)]
                ub = [Slot(pb(f"m1ub{i}", [128, TB], BF16)) for i in range(4)]
                zb = [Slot(pb(f"m1zb{i}", [128, TB], BF16)) for i in range(2)]
                steps = []
                ctr = [0]
                steps = [(j, eb) for j in range(NB) for eb in range(6)]
                state = {}

                def m1_load(idx):
                    j, eb = steps[idx]
                    if eb == 0:
                        load_xblock(xin[j % len(xin)], xTb, "xTb", j)
                    w = next_wt()
                    load_wtile(w, "w_in", l, eb, 8192)
                    state[idx] = w

                def m1_compute(idx):
                    j, eb = steps[idx]
                    w = state.pop(idx)
                    xs = xin[j % len(xin)]
                    w3 = w.t[:, :].rearrange("p (k n) -> p k n", n=512)
                    bk = []
                    for ci in range(4):
                        bi = next_bank()
                        bk.append(bi)
                        K.mm_group(bank_ap(bi), [(w3[:, kc, ci * 128:(ci + 1) * 128], xs.t[:, kc, :]) for kc in range(16)],
                                   reads=[w.b, xs.b], writes=[banks[bi].b])
                    tsl = slice(j * TB, (j + 1) * TB)
                    if l == 0 and idx % 2 == 0:
                        pump(1, "pe")
                    if eb < 4:
                        for half in range(2):
                            c = eb * 2 + half
                            bv, bg = bk[2 * half], bk[2 * half + 1]
                            sg = sig[ctr[0] % 2]
                            u = ub[ctr[0] % 4]
                            ctr[0] += 1
                            K.op("act", lambda e, sg=sg, bg=bg, c=c: e.activation(out=sg.t[:], in_=bank_ap(bg), func=AF.Sigmoid,
                                                                                 bias=pcol(l, "b_in", 2 * c + 1), scale=1.0),
                                 reads=[banks[bg].b, cbuf], writes=[sg.b])
                            K.op("dve", lambda e, sg=sg, u=u, bv=bv, c=c: e.scalar_tensor_tensor(
                                u.t[:], bank_ap(bv), pcol(l, "b_in", 2 * c), sg.t[:], ALU.add, ALU.mult),
                                reads=[banks[bv].b, sg.b, cbuf], writes=[u.b])
                            K.dma("pool", uT[c * 128:(c + 1) * 128, tsl], u.t[:], reads=[u.b], writes=[db("uT", c, j)])
                    elif eb == 4:
                        for ci in range(4):
                            u = uo[ctr[0] % 4]
                            ctr[0] += 1
                            K.op("act", lambda e, u=u, bi=bk[ci], ci=ci: e.activation(out=u.t[:], in_=bank_ap(bi), func=AF.Identity,
                                                                                     bias=pcol(l, "b_in", 16 + ci), scale=1.0),
                                 reads=[banks[bk[ci]].b, cbuf], writes=[u.b])
                            K.dma("act", zpT[ci * 128:(ci + 1) * 128, tsl], u.t[:], reads=[u.b], writes=[db("zpT", ci, j)])
                    else:
                        for ci in range(4):
                            z = zb[ctr[0] % 2]
                            ctr[0] += 1
                            K.op("act", lambda e, z=z, bi=bk[ci], ci=ci: e.activation(out=z.t[:], in_=bank_ap(bi), func=AF.Identity,
                                                                                     bias=pcol(l, "b_in", 20 + ci), scale=1.0),
                                 reads=[banks[bk[ci]].b, cbuf], writes=[z.b])
                            K.dma("act", ufT[ci * 128:(ci + 1) * 128, tsl], z.t[:], reads=[z.b], writes=[db("ufT", ci, j)])

                run_pipeline([((lambda i=i: m1_load(i)), (lambda i=i: m1_compute(i))) for i in range(len(steps))])
                K.barrier()

            ensure_cast(("w_four", l))
            phs = ExitStack()
            ssum = Slot(phs.enter_context(nc.sbuf_tensor(f"c_ssum_L{l}", [128, S], F32)))
            ssq = Slot(phs.enter_context(nc.sbuf_tensor(f"c_ssq_L{l}", [128, S], F32)))
            with ExitStack() as ph:
                pb = lambda name, shape, dt: ph.enter_context(nc.sbuf_tensor(f"{name}_L{l}", list(shape), dt))
                PADW = S + 30
                up = [Slot(pb(f"c_up{i}", [128, PADW], BF16)) for i in range(2)]
                dg = [Slot(pb(f"c_dg{i}", [128, KW, 128], BF16)) for i in range(2)]
                acc = [Slot(pb(f"c_acc{i}", [128, S], F32)) for i in range(2)]
                sqt = [Slot(pb(f"c_sqt{i}", [128, TB], F32)) for i in range(2)]
                ssum_b = [Buf() for _ in range(NB)]
                ssq_b = [Buf() for _ in range(NB)]
                for s_ in up:
                    K.op("pool", lambda e, s_=s_: e.memset(s_.t[:, 0:15], 0.0), writes=[s_.b])
                    K.op("pool", lambda e, s_=s_: e.memset(s_.t[:, 15 + S:PADW], 0.0), writes=[s_.b])

                def m2_load(c):
                    K.dma("sp", up[c % 2].t[:, 15:15 + S], uT[c * 128:(c + 1) * 128, :], reads=dbs("uT", [c], ALLJ),
                          writes=[up[c % 2].b])
                    gl = dg[c % 2]
                    for jt in range(KW):
                        K.op("dve", lambda e, jt=jt: e.tensor_scalar(gl.t[:, jt, :], ident[:], pcol(l, "wdw", jt * 8 + c), None, ALU.mult),
                             reads=[cbuf], writes=[gl.b], mark=(jt == KW - 1))

                def m2_compute(c):
                    u = up[c % 2]
                    a = acc[c % 2]
                    g_ = dg[c % 2]
                    pump(2, "pe")
                    for tt in range(NB):
                        bi = next_bank()
                        K.mm_group(bank_ap(bi), [(g_.t[:, jt, :], u.t[:, tt * TB + jt:tt * TB + jt + TB]) for jt in range(KW)],
                                   reads=[g_.b, u.b], writes=[banks[bi].b])
                        tsl = slice(tt * TB, (tt + 1) * TB)
                        K.op("act", lambda e, bi=bi, tsl=tsl: e.activation(out=a.t[:, tsl], in_=bank_ap(bi), func=AF.Identity,
                                                                         bias=pcol(l, "b_dw", c), scale=1.0),
                             reads=[banks[bi].b, cbuf], writes=[a.b])
                        if c == 0:
                            K.op("act", lambda e, tsl=tsl: e.activation(out=ssq.t[:, tsl], in_=a.t[:, tsl], func=AF.Square),
                                 reads=[a.b], writes=[ssq_b[tt]])
                            K.op("pool", lambda e, tsl=tsl: e.tensor_copy(ssum.t[:, tsl], a.t[:, tsl]), reads=[a.b], writes=[ssum_b[tt]])
                        else:
                            q_ = sqt[tt % 2]
                            K.op("act", lambda e, tsl=tsl, q_=q_: e.activation(out=q_.t[:], in_=a.t[:, tsl], func=AF.Square),
                                 reads=[a.b], writes=[q_.b])
                            K.op("pool", lambda e, tsl=tsl, q_=q_: e.tensor_tensor(ssq.t[:, tsl], ssq.t[:, tsl], q_.t[:], ALU.add),
                                 reads=[q_.b, ssq_b[tt]], writes=[ssq_b[tt]])
                            K.op("pool", lambda e, tsl=tsl: e.tensor_tensor(ssum.t[:, tsl], ssum.t[:, tsl], a.t[:, tsl], ALU.add),
                                 reads=[a.b, ssum_b[tt]], writes=[ssum_b[tt]])
                    K.dma("act", yT[c * 128:(c + 1) * 128, :], a.t[:], reads=[a.b], writes=dbs("yT", [c], ALLJ))

                with ExitStack() as ph4:
                    pb4 = lambda name, shape, dt: ph4.enter_context(nc.sbuf_tensor(f"{name}_L{l}", list(shape), dt))
                    PW = S + 16
                    xp = Slot(pb4("p_x", [128, PW], F32))
                    wk = [Slot(pb4(f"p_w{i}", [128, PW], F32)) for i in range(2)]
                    pbf = [Slot(pb4(f"p_pb{i}", [128, S], BF16)) for i in range(2)]
                    po = [Slot(pb4(f"p_po{i}", [128, S], BF16)) for i in range(2)]
                    wpl = Slot(pb4("p_wpl", [128, 512], BF16))
                    K.dma("sp", wpl.t[:], wbf["w_pool"][l * 128:(l + 1) * 128, :], reads=[db("w_pool", l, 0)], writes=[wpl.b])
                    K.op("pool", lambda e: e.memset(xp.t[:, 0:8], 0.0), writes=[xp.b])
                    K.op("pool", lambda e: e.memset(xp.t[:, 8 + S:PW], 0.0), writes=[xp.b])

                    def m4_load(g):
                        K.dma("sp", xp.t[:, 8:8 + S], zpT[g * 128:(g + 1) * 128, :], reads=dbs("zpT", [g], ALLJ), writes=[xp.b])

                    def m4_dve(g):
                        k = POOL_K[g]
                        src = xp
                        n = PW
                        step = 1
                        i = 0
                        while step < k:
                            dst = wk[i % 2]
                            n2 = n - step
                            K.op("dve", lambda e, src=src, dst=dst, n2=n2, step=step: e.tensor_tensor(
                                dst.t[:, 0:n2], src.t[:, 0:n2], src.t[:, step:step + n2], ALU.add), reads=[src.b], writes=[dst.b])
                            src = dst
                            n = n2
                            step *= 2
                            i += 1
                        off = 8 - k // 2
                        pbuf = pbf[g % 2]
                        K.op("dve", lambda e, src=src, off=off: e.scalar_tensor_tensor(pbuf.t[:], src.t[:, off:off + S], 1.0 / k,
                                                                                      xp.t[:, 8:8 + S], ALU.mult, ALU.subtract),
                             reads=[src.b, xp.b], writes=[pbuf.b])
                        edges = [(t, t + k // 2) for t in range(k // 2)] + [(t, S - t + k // 2) for t in range(S - k // 2 + 1, S)]
                        for (t, cnt) in edges:
                            K.op("dve", lambda e, src=src, off=off, t=t, cnt=cnt: e.scalar_tensor_tensor(
                                pbuf.t[:, t:t + 1], src.t[:, off + t:off + t + 1], 1.0 / cnt, xp.t[:, 8 + t:8 + t + 1],
                                ALU.mult, ALU.subtract), reads=[src.b, xp.b, pbuf.b], writes=[pbuf.b])

                    def m4_mm(g):
                        pbuf = pbf[g % 2]
                        o = po[g % 2]
                        for jt in range(NB):
                            tsl = slice(jt * TB, (jt + 1) * TB)
                            bi = next_bank()
                            K.op("pe", lambda e, bi=bi, tsl=tsl: e.matmul(bank_ap(bi), wpl.t[:, g * 128:(g + 1) * 128], pbuf.t[:, tsl],
                                                                         start=True, stop=True),
                                 reads=[wpl.b, pbuf.b], writes=[banks[bi].b])
                            K.op("dve", lambda e, bi=bi, tsl=tsl: e.tensor_scalar(o.t[:, tsl], bank_ap(bi), pcol(l, "b_pool", g),
                                                                                 pcol(l, "pscale", g), ALU.add, ALU.mult),
                                 reads=[banks[bi].b, cbuf], writes=[o.b])
                        K.dma("pool", mixT[(8 + g) * 128:(9 + g) * 128, :], o.t[:], reads=[o.b], writes=dbs("mixT", [8 + g], ALLJ))

                    def cm_load(c):
                        m2_load(c)
                        if c % 2 == 0:
                            m4_load(c // 2)

                    def cm_compute(c):
                        m2_compute(c)
                        if c % 2 == 0:
                            m4_dve(c // 2)
                        else:
                            m4_mm(c // 2)

                    run_pipeline([((lambda c=c: cm_load(c)), (lambda c=c: cm_compute(c))) for c in range(8)], pf=1)
                    K.barrier()
                for jt in range(NB):
                    tsl = slice(jt * TB, (jt + 1) * TB)
                    b1, b2 = next_bank(), next_bank()
                    mq = sqt[jt % 2]
                    K.op("pe", lambda e, b1=b1, tsl=tsl: e.matmul(bank_ap(b1), ones[:], ssum.t[:, tsl], start=True, stop=True),
                         reads=[ssum_b[jt], cbuf], writes=[banks[b1].b])
                    K.op("pe", lambda e, b2=b2, tsl=tsl: e.matmul(bank_ap(b2), ones[:], ssq.t[:, tsl], start=True, stop=True),
                         reads=[ssq_b[jt], cbuf], writes=[banks[b2].b])
                    K.op("dve", lambda e, b1=b1, tsl=tsl: e.tensor_scalar(ssum.t[:, tsl], bank_ap(b1), 1.0 / 1024, None, ALU.mult),
                         reads=[banks[b1].b], writes=[ssum_b[jt]])
                    K.op("dve", lambda e, tsl=tsl, mq=mq: e.tensor_tensor(mq.t[:], ssum.t[:, tsl], ssum.t[:, tsl], ALU.mult),
                         reads=[ssum_b[jt]], writes=[mq.b])
                    K.op("dve", lambda e, b2=b2, tsl=tsl, mq=mq: e.scalar_tensor_tensor(ssq.t[:, tsl], bank_ap(b2), 1.0 / 1024, mq.t[:],
                                                                                     ALU.mult, ALU.subtract),
                         reads=[banks[b2].b, mq.b], writes=[ssq_b[jt]])
                    K.op("act", lambda e, tsl=tsl: e.activation(out=ssq.t[:, tsl], in_=ssq.t[:, tsl], func=AF.Sqrt,
                                                                 bias=epst[:, 0:1], scale=1.0), reads=[ssq_b[jt], cbuf], writes=[ssq_b[jt]])
                    K.op("dve", lambda e, tsl=tsl: e.reciprocal(ssq.t[:, tsl], ssq.t[:, tsl]), reads=[ssq_b[jt]], writes=[ssq_b[jt]])
                K.barrier()

            with ExitStack() as ph:
                pb = lambda name, shape, dt: ph.enter_context(nc.sbuf_tensor(f"{name}_L{l}", list(shape), dt))
                alloc_stream(ph, 2, 0)
                AB = Slot(pb("f_ab", [128, 32, 4, 256], BF16))
                zf = [Slot(pb(f"f_zf{i}", [128, TB], BF16)) for i in range(2)] * 2
                fo = [Slot(pb(f"f_fo{i}", [128, TB], BF16)) for i in range(2)] * 2
                zr = [Slot(pb(f"f_zr{i}", [128, TB], BF16)) for i in range(2)] * 2
                fr = [Slot(pb(f"f_fr{i}", [128, TB + 8], BF16)) for i in range(2)] * 2
                Esb = [Slot(pb(f"f_esb{i}", [128, TB], F32)) for i in range(4)]
                ZF = [Slot(pb(f"f_zF{i}", [128, TB], BF16)) for i in range(4)]
                ZR = [Slot(pb(f"f_zR{i}", [128, TB], BF16)) for i in range(4)]
                nyq = Slot(pb("f_nyq", [128, TB], BF16))
                zn = Slot(pb("f_zn", [128, 8], BF16))
                fn = Slot(pb("f_fn", [128, 8], BF16))
                wfo = Slot(pb("f_wfo", [128, 512], BF16))
                K.dma("sp", wfo.t[:], wbf["w_four"][l * 128:(l + 1) * 128, :], reads=[db("w_four", l, 0)], writes=[wfo.b])
                ABb = [[Buf() for _ in range(16)] for _ in range(4)]
                acc1 = Slot(pb("f_m3acc", [128, S], F32))
                ob1 = Slot(pb("f_m3ob", [128, S], BF16))

                def m3_load(c):
                    K.dma("sp", acc1.t[:], yT[c * 128:(c + 1) * 128, :], reads=dbs("yT", [c], ALLJ), writes=[acc1.b])

                def m3_compute(c):
                    a = acc1
                    o = ob1
                    K.op("dve", lambda e: e.tensor_tensor(a.t[:], a.t[:], ssum.t[:], ALU.subtract), reads=[a.b] + ssum_b, writes=[a.b])
                    K.op("pool", lambda e: e.tensor_tensor(a.t[:], a.t[:], ssq.t[:], ALU.mult), reads=[a.b] + ssq_b, writes=[a.b])
                    K.op("act", lambda e: e.activation(out=o.t[:], in_=a.t[:], func=AF.Silu, bias=pcol(l, "cb", c),
                                                       scale=pcol(l, "cg", c)), reads=[a.b, cbuf], writes=[o.b])
                    K.dma("act", mixT[c * 128:(c + 1) * 128, :], o.t[:], reads=[o.b], writes=dbs("mixT", [c], ALLJ))

                pha = ExitStack()
                ufs = [Slot(pha.enter_context(nc.sbuf_tensor(f"f_uf{i}_L{l}", [128, S], BF16))) for i in range(2)]

                def m5a_load(h):
                    K.dma("sp", ufs[h % 2].t[:], ufT[h * 128:(h + 1) * 128, :], reads=dbs("ufT", [h], ALLJ), writes=[ufs[h % 2].b])

                def m5a_compute(h):
                    u = ufs[h % 2]
                    for sp_ in range(16):
                        bi = next_bank()
                        for half in range(2):
                            st = 2 * sp_ + half
                            K.op("pe", lambda e, st=st, half=half, bi=bi: e.matmul(ps[:, bi, half * 256:(half + 1) * 256],
                                                                                  u.t[:, st * 128:(st + 1) * 128], ccsc[:],
                                                                                  start=True, stop=True),
                                 reads=[u.b, cbuf], writes=[banks[bi].b], mark=(half == 1))
                        eng = "act" if sp_ % 2 == 0 else "dve"
                        outv = AB.t[:, 2 * sp_:2 * sp_ + 2, h, :]
                        inv = ps[:, bi, :].rearrange("p (a b) -> p a b", b=256)
                        if eng == "act":
                            K.op("act", lambda e, outv=outv, inv=inv: e.activation(out=outv, in_=inv, func=AF.Copy),
                                 reads=[banks[bi].b], writes=[ABb[h][sp_]])
                        else:
                            K.op("dve", lambda e, outv=outv, inv=inv: e.tensor_copy(outv, inv), reads=[banks[bi].b], writes=[ABb[h][sp_]])

                run_pipeline([((lambda h=h: m5a_load(h)), (lambda h=h: m5a_compute(h))) for h in range(4)], pf=1)
                pha.close()
                K.barrier(["sp"])
                wt.append(Slot(ph.enter_context(nc.sbuf_tensor(f"wt3_L{l}", [128, 16 * 512], BF16))))

                ABall = [b_ for h_ in range(4) for b_ in ABb[h_]]
                Imat = ijc[:, 0:128]
                Jmat = ijc[:, 128:256]
                bn = next_bank()
                for st in range(32):
                    K.op("pe", lambda e, st=st: e.matmul(ps[0:1, bn, :].rearrange("p (a b) -> p a b", b=128), ijc[:, 256:257], AB.t[:, st, :, 0:128],
                                                         start=(st == 0), stop=(st == 31)),
                         reads=ABall + [cbuf], writes=[banks[bn].b], mark=(st == 31))
                K.op("act", lambda e: e.activation(out=nyq.t[0:1, :], in_=ps[0:1, bn, :], func=AF.Copy), reads=[banks[bn].b], writes=[nyq.b])
                for h in range(4):
                    b1 = next_bank()
                    K.op("pe", lambda e, h=h, b1=b1: e.matmul(ps[:, b1, 0:1], nyq.t[0:1, h * 128:(h + 1) * 128], ijc[0:1, 0:1],
                                                              start=True, stop=True), reads=[nyq.b, cbuf], writes=[banks[b1].b])
                    K.op("act", lambda e, h=h, b1=b1: e.activation(out=zn.t[:, h:h + 1], in_=ps[:, b1, 0:1], func=AF.Copy),
                         reads=[banks[b1].b], writes=[zn.b])
                    b2 = next_bank()
                    K.op("pe", lambda e, h=h, b2=b2: e.matmul(ps[:, b2, 0:1], wfo.t[:, h * 128:(h + 1) * 128], zn.t[:, h:h + 1],
                                                              start=True, stop=True), reads=[wfo.b, zn.b], writes=[banks[b2].b])
                    K.op("act", lambda e, h=h, b2=b2: e.activation(out=fn.t[:, h:h + 1], in_=ps[:, b2, 0:1], func=AF.Identity,
                                                                  bias=pcol(l, "b_f", h), scale=1.0),
                         reads=[banks[b2].b, cbuf], writes=[fn.b])

                st5 = {}
                bsteps = [(kt, part) for kt in range(4) for part in range(4)]
                EBK = (0, 1, 2, 3)
                OBK = (4, 5, 6, 7)
                pctr = [0]

                def pbank():
                    i = OBK[pctr[0] % 4]
                    pctr[0] += 1
                    return i

                def m5b_load(idx):
                    kt, part = bsteps[idx]
                    w = next_wt()
                    r0 = (kt * 4 + part) * 128
                    K.dma("sp", w.t[:, :], dft_d[r0:r0 + 128, :], writes=[w.b])
                    st5[idx] = w
                    if idx % 2 == 1:
                        m3_load(idx // 2)

                def post_T(kt, hs):
                    bl = []
                    for h in hs:
                        for (Z, mat, zt, rev) in ((ZF, Imat, zf[h], False), (ZR, Jmat, zr[h], True)):
                            bt = EBK[len(bl) % 4]
                            bl.append(bt)
                            for kq in range(4):
                                col = (3 - kq) if rev else kq
                                K.op("pe", lambda e, Z=Z, mat=mat, kq=kq, col=col, bt=bt, h=h: e.matmul(
                                    ps[:, bt, col * 128:(col + 1) * 128], Z[kq].t[:, h * 128:(h + 1) * 128], mat, start=True, stop=True),
                                    reads=[Z[kq].b, cbuf], writes=[banks[bt].b], mark=(kq == 3))
                            K.op("act", lambda e, zt=zt, bt=bt: e.activation(out=zt.t[:], in_=bank_ap(bt), func=AF.Copy),
                                 reads=[banks[bt].b], writes=[zt.b])

                def post_C(kt, hs):
                    n = 0
                    for h in hs:
                        for (zt, ot, rev) in ((zf[h], fo[h], False), (zr[h], fr[h], True)):
                            b2 = EBK[n % 4]
                            n += 1
                            K.op("pe", lambda e, zt=zt, b2=b2, h=h: e.matmul(bank_ap(b2), wfo.t[:, h * 128:(h + 1) * 128], zt.t[:],
                                                                            start=True, stop=True),
                                 reads=[wfo.b, zt.b], writes=[banks[b2].b])
                            rows = slice((12 + h) * 128, (13 + h) * 128)
                            if not rev:
                                K.op("act", lambda e, ot=ot, b2=b2, h=h: e.activation(out=ot.t[:], in_=bank_ap(b2), func=AF.Identity,
                                                                                    bias=pcol(l, "b_f", h), scale=1.0),
                                     reads=[banks[b2].b, cbuf], writes=[ot.b])
                                K.dma("act", mixT[rows, kt * TB:(kt + 1) * TB], ot.t[:], reads=[ot.b], writes=[db("mixT", 12 + h, kt)])
                            else:
                                K.op("act", lambda e, ot=ot, b2=b2, h=h: e.activation(out=ot.t[:, 1:1 + TB], in_=bank_ap(b2), func=AF.Identity,
                                                                                    bias=pcol(l, "b_f", h), scale=1.0),
                                     reads=[banks[b2].b, cbuf], writes=[ot.b])
                                c0 = (7 - kt) * TB + 1
                                if kt == 3:
                                    K.op("act", lambda e, ot=ot, h=h: e.activation(out=ot.t[:, 0:1], in_=fn.t[:, h:h + 1], func=AF.Copy),
                                         reads=[fn.b, ot.b], writes=[ot.b])
                                    K.dma("act", mixT[rows, c0 - 1:c0 + TB], ot.t[:, 0:TB + 1], reads=[ot.b],
                                          writes=[db("mixT", 12 + h, 4), db("mixT_c0", 12 + h, 5)])
                                elif kt == 0:
                                    K.dma("act", mixT[rows, c0:c0 + TB - 1], ot.t[:, 1:TB], reads=[ot.b], writes=[db("mixT", 12 + h, 7)])
                                else:
                                    K.dma("act", mixT[rows, c0:c0 + TB], ot.t[:, 1:1 + TB], reads=[ot.b],
                                          writes=[db("mixT", 12 + h, 7 - kt), db("mixT_c0", 12 + h, 8 - kt)])

                def m5b_compute(idx):
                    kt, part = bsteps[idx]
                    w = st5.pop(idx)
                    w3 = w.t[:, :].rearrange("p (k n) -> p k n", n=512)
                    ab, half = part // 2, part % 2
                    if idx % 2 == 0:
                        m3_compute(idx // 2)
                    if part >= 2 and kt > 0:
                        post_T(kt - 1, (0, 1) if part == 2 else (2, 3))
                    bks = EBK if ab == 0 else OBK
                    for kq in range(4):
                        bi = bks[kq]
                        for s16 in range(16):
                            st = half * 16 + s16
                            K.op("pe", lambda e, bi=bi, st=st, kq=kq, s16=s16: e.matmul(
                                ps[:, bi, :].rearrange("p (a b) -> p a b", b=128), w3[:, s16, kq * 128:(kq + 1) * 128],
                                AB.t[:, st, :, ab * 128:(ab + 1) * 128],
                                start=(half == 0 and s16 == 0), stop=(half == 1 and s16 == 15)),
                                reads=[w.b] + ABall, writes=[banks[bi].b], mark=(s16 == 15))
                    if part >= 2 and kt > 0:
                        post_C(kt - 1, (0, 1) if part == 2 else (2, 3))
                    if part == 1:
                        for kq in range(4):
                            K.op("act", lambda e, kq=kq: e.activation(out=Esb[kq].t[:], in_=bank_ap(EBK[kq]), func=AF.Copy),
                                 reads=[banks[EBK[kq]].b], writes=[Esb[kq].b])
                    if part == 3:
                        for kq in range(4):
                            K.op("dve", lambda e, kq=kq: e.tensor_tensor(ZF[kq].t[:], Esb[kq].t[:], bank_ap(OBK[kq]), ALU.add),
                                 reads=[Esb[kq].b, banks[OBK[kq]].b], writes=[ZF[kq].b])
                            K.op("dve", lambda e, kq=kq: e.tensor_tensor(ZR[kq].t[:], Esb[kq].t[:], bank_ap(OBK[kq]), ALU.subtract),
                                 reads=[Esb[kq].b, banks[OBK[kq]].b], writes=[ZR[kq].b])

                run_pipeline([((lambda i=i: m5b_load(i)), (lambda i=i: m5b_compute(i))) for i in range(len(bsteps))], pf=2)
                post_T(3, (0, 1))
                post_C(3, (0, 1))
                post_T(3, (2, 3))
                post_C(3, (2, 3))
                K.barrier()
            phs.close()

            ensure_cast(("w_out", l))
            with ExitStack() as ph:
                pb = lambda name, shape, dt: ph.enter_context(nc.sbuf_tensor(f"{name}_L{l}", list(shape), dt))
                alloc_stream(ph, 3, 2)
                hb = [Slot(pb(f"o_hb{i}", [128, 16, TB], F32)) for i in range(2)]
                hbb = [[Buf() for _ in range(16)] for _ in range(2)]
                s1 = [Slot(pb(f"o_s1{i}", [128, TB], F32)) for i in range(2)]
                s2 = [Slot(pb(f"o_s2{i}", [128, TB], F32)) for i in range(2)]
                xr = [Slot(pb(f"o_xr{i}", [128, 4, TB], F32)) for i in range(3)]
                hsq = [Slot(pb(f"o_hsq{i}", [128, TB], F32)) for i in range(2)]
                tiles = {"mean": [Slot(pb(f"o_mean{i}", [128, TB], F32)) for i in range(2)], "msq": Slot(pb("o_msq", [128, TB], F32)),
                         "rstd": [Slot(pb(f"o_rstd{i}", [128, TB], F32)) for i in range(2)],
                         "of": [Slot(pb(f"o_of{i}", [128, TB], F32)) for i in range(2)],
                         "ob": [Slot(pb(f"o_ob{i}", [128, TB], BF16)) for i in range(2)]}
                osteps = [(j, eb) for j in range(NB) for eb in range(4)]
                sto = {}

                def o_load(idx):
                    j, eb = osteps[idx]
                    if eb == 0:
                        load_xblock(xin[j % len(xin)], mixT, "mixT", j)
                    w = next_wt()
                    load_wtile(w, "w_out", l, eb, 8192)
                    x_ = xr[idx % 3]
                    rd = [] if xres_name is None else dbs(xres_name, range(eb * 4, eb * 4 + 4), [j])
                    K.dma("sp", x_.t[:], xres_src[eb * 512:(eb + 1) * 512, j * TB:(j + 1) * TB].rearrange("(c p) t -> p c t", p=128),
                          reads=rd, writes=[x_.b])
                    sto[idx] = (w, x_)

                def o_compute(idx):
                    j, eb = osteps[idx]
                    w, x_ = sto.pop(idx)
                    xs = xin[j % len(xin)]
                    h_ = hb[j % 2]
                    w3 = w.t[:, :].rearrange("p (k n) -> p k n", n=512)
                    bk = []
                    for ci in range(4):
                        bi = next_bank()
                        bk.append(bi)
                        K.mm_group(bank_ap(bi), [(w3[:, kc, ci * 128:(ci + 1) * 128], xs.t[:, kc, :]) for kc in range(16)],
                                   reads=[w.b, xs.b], writes=[banks[bi].b])
                    hbj = hbb[j % 2]
                    for ci in range(4):
                        dc = eb * 4 + ci
                        hv = h_.t[:, dc, :]
                        K.op("dve", lambda e, hv=hv, bi=bk[ci], ci=ci: e.scalar_tensor_tensor(hv, x_.t[:, ci, :], ALPHA, bank_ap(bi),
                                                                                             ALU.mult, ALU.add),
                             reads=[x_.b, banks[bk[ci]].b], writes=[hbj[dc]])
                        K.op("dve", lambda e, hv=hv, dc=dc: e.tensor_scalar(hv, hv, pcol(l, "b_out", dc), None, ALU.add),
                             reads=[hbj[dc], cbuf], writes=[hbj[dc]])
                    run_pending(5)
                    for ci in range(4):
                        dc = eb * 4 + ci
                        stats_chunk(h_, hbj, dc, hsq[dc % 2], s1[j % 2], s2[j % 2], sq_on_pool=True)
                    if eb == 3:
                        pending.extend(layer_norm_block(l, j, h_, hbj, "g1", "b1", s1[j % 2], s2[j % 2], tiles, xTf, "xTf", False))

                run_pipeline([((lambda i=i: o_load(i)), (lambda i=i: o_compute(i))) for i in range(len(osteps))])
                run_pending(100)
                K.barrier()

            for lab in ("pT", "w_gu", "w_pg", "w_ple", "w_down"):
                ensure_cast((lab, l))
            with ExitStack() as ph:
                pb = lambda name, shape, dt: ph.enter_context(nc.sbuf_tensor(f"{name}_L{l}", list(shape), dt))
                alloc_stream(ph, 3, 1)
                hid = Slot(pb("f_hid", [128, NF, TB], BF16))
                hb = [Slot(pb("f_hb0", [128, 16, TB], F32))]
                hbb = [Buf() for _ in range(16)]
                s1 = Slot(pb("f_s1", [128, TB], F32))
                s2 = Slot(pb("f_s2", [128, TB], F32))
                xr = [Slot(pb(f"f_xr{i}", [128, 4, TB], F32)) for i in range(2)]
                pts = [Slot(pb(f"f_pt{i}", [128, 2, TB], BF16)) for i in range(2)]
                wple = Slot(pb("f_wple", [128, 4, 1024], BF16))
                sg = [Slot(pb(f"f_sg{i}", [128, TB], F32)) for i in range(2)]
                pl = [Slot(pb(f"f_pl{i}", [128, TB], F32)) for i in range(2)]
                hsq = [Slot(pb(f"f_hsq{i}", [128, TB], F32)) for i in range(1)] * 2
                tiles = {"mean": [Slot(pb(f"f_mean{i}", [128, TB], F32)) for i in range(2)], "msq": Slot(pb("f_msq", [128, TB], F32)),
                         "rstd": [Slot(pb(f"f_rstd{i}", [128, TB], F32)) for i in range(2)],
                         "of": [Slot(pb(f"f_of{i}", [128, TB], F32)) for i in range(2)],
                         "ob": [Slot(pb(f"f_ob{i}", [128, TB], BF16)) for i in range(2)]}
                for eb in range(4):
                    r0 = (l * 4 + eb) * 128
                    K.dma("sp", wple.t[:, eb, :], wbf["w_ple"][r0:r0 + 128, :], reads=[db("w_ple", l, eb)], writes=[wple.b])
                fsteps = []
                for j in range(NB):
                    for t in range(22):
                        fsteps.append((j, "gu", t, 0))
                    for eb in range(4):
                        fsteps.append((j, "pg", eb, 0))
                        for kg in range(4):
                            fsteps.append((j, "dn", eb, kg))
                stf = {}
                sgc = [0]
                dst_f32, dst_name = (outT, "outT") if last_layer else (xTf, "xTf")

                def f_load(idx):
                    j, kind, a, kg = fsteps[idx]
                    if kind == "gu" and a == 0:
                        load_xblock(xin[j % len(xin)], xTb, "xTb", j)
                        pt = pts[j % 2]
                        r0 = l * PLE
                        K.dma("sp", pt.t[:], pTb[r0:r0 + PLE, j * TB:(j + 1) * TB].rearrange("(c p) t -> p c t", p=128),
                              reads=dbs("pTb", [(l, 0), (l, 1)], [j]), writes=[pt.b])
                    w = next_wt()
                    x_ = None
                    if kind == "gu":
                        load_wtile(w, "w_gu", l, a, 8192)
                    elif kind == "pg":
                        load_wtile(w, "w_pg", l, a, 8192)
                        x_ = xr[a % 2]
                        K.dma("sp", x_.t[:], xTf[a * 512:(a + 1) * 512, j * TB:(j + 1) * TB].rearrange("(c p) t -> p c t", p=128),
                              reads=dbs("xTf", range(a * 4, a * 4 + 4), [j]), writes=[x_.b])
                    else:
                        load_wtile(w, "w_down", l, a * 4 + kg, 11 * 512)
                    stf[idx] = (w, x_)

                def f_compute(idx):
                    j, kind, a, kg = fsteps[idx]
                    w, x_ = stf.pop(idx)
                    xs = xin[j % len(xin)]
                    h_ = hb[0]
                    if kind == "gu":
                        w3 = w.t[:, :].rearrange("p (k n) -> p k n", n=512)
                        bk = []
                        for ci in range(4):
                            bi = next_bank()
                            bk.append(bi)
                            K.mm_group(bank_ap(bi), [(w3[:, kc, ci * 128:(ci + 1) * 128], xs.t[:, kc, :]) for kc in range(16)],
                                       reads=[w.b, xs.b], writes=[banks[bi].b])
                        for half in range(2):
                            f = 2 * a + half
                            bg, bu = bk[2 * half], bk[2 * half + 1]
                            s_ = sg[sgc[0] % 2]
                            sgc[0] += 1
                            K.op("act", lambda e, s_=s_, bg=bg: e.activation(out=s_.t[:], in_=bank_ap(bg), func=AF.Silu),
                                 reads=[banks[bg].b], writes=[s_.b])
                            K.op("dve", lambda e, s_=s_, bu=bu, f=f: e.tensor_tensor(hid.t[:, f, :], s_.t[:], bank_ap(bu), ALU.mult),
                                 reads=[s_.b, banks[bu].b], writes=[hid.b])
                    elif kind == "pg":
                        eb = a
                        w3 = w.t[:, :].rearrange("p (k n) -> p k n", n=512)
                        pgpool = (4, 5, 6, 7) if eb % 2 == 0 else (0, 1, 2, 3)
                        pt = pts[j % 2]
                        for ci in range(4):
                            dc = eb * 4 + ci
                            bp = next_bank(pgpool)
                            bq = next_bank(pgpool)
                            K.mm_group(bank_ap(bp), [(w3[:, kc, ci * 128:(ci + 1) * 128], xs.t[:, kc, :]) for kc in range(16)],
                                       reads=[w.b, xs.b], writes=[banks[bp].b])
                            K.mm_group(bank_ap(bq), [(wple.t[:, eb, kc * 512 + ci * 128:kc * 512 + (ci + 1) * 128], pt.t[:, kc, :])
                                                     for kc in range(2)], reads=[wple.b, pt.b], writes=[banks[bq].b])
                            s_ = sg[sgc[0] % 2]
                            sgc[0] += 1
                            p_ = pl[dc % 2]
                            K.op("act", lambda e, s_=s_, bp=bp: e.activation(out=s_.t[:], in_=bank_ap(bp), func=AF.Sigmoid),
                                 reads=[banks[bp].b], writes=[s_.b])
                            K.op("dve", lambda e, s_=s_, p_=p_, bq=bq: e.tensor_tensor(p_.t[:], s_.t[:], bank_ap(bq), ALU.mult),
                                 reads=[s_.b, banks[bq].b], writes=[p_.b])
                            hv = h_.t[:, dc, :]
                            K.op("dve", lambda e, hv=hv, p_=p_, ci=ci: e.scalar_tensor_tensor(hv, x_.t[:, ci, :], ALPHA, p_.t[:],
                                                                                            ALU.mult, ALU.add),
                                 reads=[x_.b, p_.b], writes=[hbb[dc]])
                    else:
                        eb = a
                        w3 = w.t[:, 0:11 * 512].rearrange("p (k n) -> p k n", n=512)
                        dpool = (0, 1, 2, 3) if eb % 2 == 0 else (4, 5, 6, 7)
                        for ci in range(4):
                            bi = dpool[ci]
                            for kc in range(11):
                                lastk = (kc == 10)
                                mk = (lastk and (kg == 3 or ci == 3))
                                K.op("pe", lambda e, bi=bi, ci=ci, kc=kc: e.matmul(
                                    bank_ap(bi), w3[:, kc, ci * 128:(ci + 1) * 128], hid.t[:, kg * 11 + kc, :],
                                    start=(kg == 0 and kc == 0), stop=(kg == 3 and kc == 10)),
                                    reads=[w.b, hid.b], writes=[banks[bi].b], mark=mk)
                        if kg == 3:
                            for ci in range(4):
                                dc = eb * 4 + ci
                                bi = dpool[ci]
                                hv = h_.t[:, dc, :]
                                K.op("dve", lambda e, hv=hv, bi=bi: e.tensor_tensor(hv, hv, bank_ap(bi), ALU.add),
                                     reads=[hbb[dc], banks[bi].b], writes=[hbb[dc]])
                                stats_chunk(h_, hbb, dc, hsq[dc % 2], s1, s2)
                            if eb == 3:
                                pending.extend(layer_norm_block(l, j, h_, hbb, "g2", "b2", s1, s2, tiles, dst_f32, dst_name, last_layer))
                    if kind == "gu":
                        run_pending(1)
                    if idx % 4 == 0:
                        pump(1, "pe")

                run_pipeline([((lambda i=i: f_load(i)), (lambda i=i: f_compute(i))) for i in range(len(fsteps))])
                run_pending(100)
                K.barrier()

        K.barrier(["sp"])
        build.nins = K.nins
    return nc


def _block(w, kcg):
    Kd, E = w.shape
    KG = Kd // (128 * kcg)
    EB = E // 512
    return np.ascontiguousarray(w.reshape(KG, kcg, 128, EB, 512).transpose(3, 0, 2, 1, 4)).reshape(EB * KG * 128, kcg * 512)


def _cols(v):
    return np.ascontiguousarray(v.reshape(-1, 128).T)


def _prep_shared(inp, n_layers=L):
    f32 = np.float32
    out = {}
    w_in_l, w_out_l, w_gu_l, w_down_l, w_pg_l, w_ple_l, w_pool_l, w_four_l = [], [], [], [], [], [], [], []
    params = np.zeros((128, L * NPC), f32)
    order = []
    for c in range(8):
        order += [c, 8 + c]
    order += list(range(16, 24))
    for l in range(L):
        w_in = np.asarray(inp["w_in"][l], f32).reshape(D, 24, 128)[:, order, :].reshape(D, 3072)
        w_in_l.append(_block(w_in, 16))
        w_out_l.append(_block(np.asarray(inp["w_out"][l], f32), 16))
        g = np.asarray(inp["w_gate"][l], f32).reshape(D, NF, 128)
        u = np.asarray(inp["w_up"][l], f32).reshape(D, NF, 128)
        gu = np.stack([g, u], axis=2).reshape(D, 2 * DFF)
        w_gu_l.append(_block(gu, 16))
        w_down_l.append(_block(np.asarray(inp["w_down"][l], f32), 11))
        w_pg_l.append(_block(np.asarray(inp["w_ple_gate"][l], f32), 16))
        w_ple_l.append(_block(np.asarray(inp["w_ple"][l], f32), 2))
        w_pool_l.append(np.ascontiguousarray(np.asarray(inp["w_pool"][l], f32).transpose(1, 0, 2)).reshape(128, 512))
        w_four_l.append(np.ascontiguousarray(np.asarray(inp["w_fourier"][l], f32).transpose(1, 0, 2)).reshape(128, 512))
        P = params[:, l * NPC:(l + 1) * NPC]
        b_in = np.asarray(inp["b_in"][l], f32).reshape(24, 128)[order, :].reshape(-1)
        P[:, PC["b_in"]:PC["b_in"] + 24] = _cols(b_in)
        P[:, PC["b_dw"]:PC["b_dw"] + 8] = _cols(np.asarray(inp["b_dw"][l], f32))
        P[:, PC["cg"]:PC["cg"] + 8] = _cols(np.asarray(inp["conv_ln_g"][l], f32))
        P[:, PC["cb"]:PC["cb"] + 8] = _cols(np.asarray(inp["conv_ln_b"][l], f32))
        P[:, PC["b_pool"]:PC["b_pool"] + 4] = _cols(np.asarray(inp["b_pool"][l], f32).reshape(-1))
        P[:, PC["pscale"]:PC["pscale"] + 4] = _cols(np.asarray(inp["pool_scale"][l], f32))
        P[:, PC["b_f"]:PC["b_f"] + 4] = _cols(np.asarray(inp["b_fourier"][l], f32).reshape(-1))
        P[:, PC["b_out"]:PC["b_out"] + 16] = _cols(np.asarray(inp["b_out"][l], f32))
        P[:, PC["g1"]:PC["g1"] + 16] = _cols(np.asarray(inp["ln1_g"][l], f32))
        P[:, PC["b1"]:PC["b1"] + 16] = _cols(np.asarray(inp["ln1_b"][l], f32))
        P[:, PC["g2"]:PC["g2"] + 16] = _cols(np.asarray(inp["ln2_g"][l], f32))
        P[:, PC["b2"]:PC["b2"] + 16] = _cols(np.asarray(inp["ln2_b"][l], f32))
        wdw = np.asarray(inp["w_dw"][l], f32)
        for jt in range(KW):
            P[:, PC["wdw"] + jt * 8:PC["wdw"] + jt * 8 + 8] = _cols(wdw[jt])
    out["w_in"] = np.concatenate(w_in_l, 0)
    out["w_out"] = np.concatenate(w_out_l, 0)
    out["w_gu"] = np.concatenate(w_gu_l, 0)
    out["w_down"] = np.concatenate(w_down_l, 0)
    out["w_pg"] = np.concatenate(w_pg_l, 0)
    out["w_ple"] = np.concatenate(w_ple_l, 0)
    out["w_pool"] = np.concatenate(w_pool_l, 0)
    out["w_four"] = np.concatenate(w_four_l, 0)
    out["params"] = params
    idx = np.arange(S, dtype=np.int64)
    ang = (2.0 * np.pi / S) * ((idx[:, None] * idx[None, :]) % S).astype(np.float64)
    cs = (np.cos(ang) / 64.0)
    ss = (-np.sin(ang) / 64.0)
    dft = np.empty((8, 4, 128, 16, 512), ml_dtypes.bfloat16)
    for part, m in enumerate((cs, cs, ss, ss)):
        half = part % 2
        blk = m[half * 2048:(half + 1) * 2048, :].reshape(16, 128, 8, 512)
        dft[:, part] = blk.transpose(2, 1, 0, 3).astype(ml_dtypes.bfloat16)
    out["dft"] = np.ascontiguousarray(dft[0:4]).reshape(4 * 4 * 128, 16 * 512)
    c = np.arange(128)
    a2 = (2.0 * np.pi / 128) * ((c[:, None] * c[None, :]) % 128)
    sc = 1.0 / np.sqrt(128.0)
    out["ccsc"] = np.concatenate([np.cos(a2) * sc, np.sin(a2) * sc], axis=1).astype(ml_dtypes.bfloat16)
    out["ident"] = np.eye(128, dtype=f32)
    ijc = np.zeros((128, 264), f32)
    ijc[:, 0:128] = np.eye(128)
    ijc[:, 128:256] = np.eye(128)[::-1]
    ijc[:, 256] = ((-1.0) ** np.arange(128)) / 64.0
    out["ijc"] = ijc.astype(ml_dtypes.bfloat16)
    return out


_NC_CACHE = {}


def kernel(**inputs):
    n = 8
    shared = _prep_shared(inputs)
    x = np.asarray(inputs["x"], np.float32)
    p = np.asarray(inputs["p"], np.float32)
    in_maps = []
    for b in range(n):
        m = dict(shared)
        m["xT"] = np.ascontiguousarray(x[b].T)
        m["pT"] = np.ascontiguousarray(p[:, b].transpose(0, 2, 1)).reshape(L * PLE, S)
        in_maps.append(m)
    if "nc" not in _NC_CACHE:
        _NC_CACHE["nc"] = build()
    res = run_bass_kernel_spmd(_NC_CACHE["nc"], in_maps, core_ids=list(range(n)))
    out = np.empty((n, S, D), np.float32)
    for b in range(n):
        out[b] = np.asarray(res.results[b]["outT"]).T
    return out
```
